# Optimizing a Trainium2 kernel written in Bass

```python
import math
import jax, jax.numpy as jnp
from jax import lax
import numpy as np

D_MODEL = 4096
BATCH = 1
SEQ = 8192
DEPTH = 4

D_CONV = D_MODEL // 2
CONV_WIDTH = 31
N_HEADS = D_MODEL // 128
QK_NOPE_DIM = 128
QK_ROPE_DIM = 64
QK_DIM = QK_NOPE_DIM + QK_ROPE_DIM
V_DIM = 128
Q_LORA = 1536
KV_LORA = 512
D_MLA = N_HEADS * V_DIM
ROPE_THETA = 10000.0
Q_BLOCK = 128
D_LRU = D_MODEL // 2
LRU_BLOCKS = 16
LRU_BLOCK_DIM = D_LRU // LRU_BLOCKS
LRU_CONV_WIDTH = 4
LRU_C = 8.0
N_BRANCH = 3
EPS = 1e-6

IN_SPLITS = (2 * D_CONV, D_CONV,
             Q_LORA, KV_LORA, QK_ROPE_DIM, D_MLA,
             D_LRU, D_LRU,
             N_BRANCH * D_MODEL)
D_IN = sum(IN_SPLITS)

kernel_name = 'hybrid_gated_conv_mla_rglru'


def rms_norm(x, g):
    xf = x.astype(jnp.float32)
    y = xf * lax.rsqrt(jnp.mean(xf * xf, axis=-1, keepdims=True) + EPS)
    return (y * g.astype(jnp.float32)).astype(x.dtype)


def layer_norm(x, g, b):
    xf = x.astype(jnp.float32)
    mu = jnp.mean(xf, axis=-1, keepdims=True)
    var = jnp.mean(jnp.square(xf - mu), axis=-1, keepdims=True)
    y = (xf - mu) * lax.rsqrt(var + EPS)
    return (y * g.astype(jnp.float32) + b.astype(jnp.float32)).astype(x.dtype)


def causal_depthwise_conv(x, w, b):
    width, chans = w.shape
    y = lax.conv_general_dilated(x, w[:, None, :].astype(x.dtype), window_strides=(1,),
                                 padding=[(width - 1, 0)],
                                 dimension_numbers=('NWC', 'WIO', 'NWC'),
                                 feature_group_count=chans)
    return y + b


def rope(x, cos, sin):
    x1, x2 = jnp.split(x, 2, axis=-1)
    return jnp.concatenate([x1 * cos - x2 * sin, x1 * sin + x2 * cos], axis=-1)


def conformer_branch(u_glu, gate, dw_w, dw_b, ln_g, ln_b, w_proj):
    a, b = jnp.split(u_glu, 2, axis=-1)
    y = a * jax.nn.sigmoid(b)
    y = causal_depthwise_conv(y, dw_w, dw_b)
    y = layer_norm(y, ln_g, ln_b)
    y = jax.nn.silu(y) * jax.nn.silu(gate)
    return y @ w_proj


def causal_block_attention(q, k, v):
    B, S, H, _ = q.shape
    nb = S // Q_BLOCK
    scale = QK_DIM ** -0.5
    qb = q.reshape(B, nb, Q_BLOCK, H, QK_DIM).swapaxes(0, 1)
    k_pos = jnp.arange(S)
    neg = jnp.finfo(jnp.float32).min

    def one_block(args):
        q_blk, i = args
        s = jnp.einsum('bqhd,bkhd->bhqk', q_blk, k, preferred_element_type=jnp.float32) * scale
        q_pos = i * Q_BLOCK + jnp.arange(Q_BLOCK)
        s = jnp.where(k_pos[None, :] <= q_pos[:, None], s, neg)
        p = jax.nn.softmax(s, axis=-1).astype(v.dtype)
        return jnp.einsum('bhqk,bkhd->bqhd', p, v)

    o = lax.map(one_block, (qb, jnp.arange(nb)))
    return o.swapaxes(0, 1).reshape(B, S, H * V_DIM)


def mla_branch(c_q, c_kv, k_r, gate, cos, sin, q_norm_g, w_uq, kv_norm_g, w_ukv, w_proj):
    B, S, _ = c_q.shape
    q = (rms_norm(c_q, q_norm_g) @ w_uq).reshape(B, S, N_HEADS, QK_DIM)
    q_nope, q_rope = jnp.split(q, [QK_NOPE_DIM], axis=-1)
    q = jnp.concatenate([q_nope, rope(q_rope, cos[:, :, None, :], sin[:, :, None, :])], axis=-1)
    kv = (rms_norm(c_kv, kv_norm_g) @ w_ukv).reshape(B, S, N_HEADS, QK_NOPE_DIM + V_DIM)
    k_nope, v = jnp.split(kv, [QK_NOPE_DIM], axis=-1)
    k_rope = rope(k_r, cos, sin)[:, :, None, :]
    k = jnp.concatenate([k_nope, jnp.broadcast_to(k_rope, (B, S, N_HEADS, QK_ROPE_DIM))], axis=-1)
    o = causal_block_attention(q, k, v)
    return (o * jax.nn.silu(gate)) @ w_proj


def rg_lru_branch(u, gate, cw, cb, w_a, b_a, w_x, b_x, lam, w_proj):
    B, S, _ = u.shape
    xc = causal_depthwise_conv(u, cw, cb)
    xb = xc.reshape(B, S, LRU_BLOCKS, LRU_BLOCK_DIM)
    r = jax.nn.sigmoid(jnp.einsum('bsgi,gij->bsgj', xb, w_a).reshape(B, S, D_LRU) + b_a)
    i = jax.nn.sigmoid(jnp.einsum('bsgi,gij->bsgj', xb, w_x).reshape(B, S, D_LRU) + b_x)
    log_a = -LRU_C * r.astype(jnp.float32) * jax.nn.softplus(-lam.astype(jnp.float32))
    a = jnp.exp(log_a)
    mult = jnp.sqrt(-jnp.expm1(2.0 * log_a))
    b = mult * (i * xc).astype(jnp.float32)

    def combine(c1, c2):
        a1, b1 = c1
        a2, b2 = c2
        return a1 * a2, a2 * b1 + b2

    _, h = lax.associative_scan(combine, (a, b), axis=1)
    y = h.astype(u.dtype) * jax.nn.silu(gate)
    return y @ w_proj


def setup_inputs(seed: int = 0) -> dict:
    key = jax.random.key(seed)
    ks = jax.random.split(key, 32)
    f32 = jnp.float32

    def w(k, shape, fan_in):
        return jax.random.normal(k, shape, f32) * (fan_in ** -0.5)

    def gain(k, shape):
        return 1.0 + 0.02 * jax.random.normal(k, shape, f32)

    def bias(k, shape):
        return 0.01 * jax.random.normal(k, shape, f32)

    u = jax.random.uniform(ks[20], (DEPTH, D_LRU), f32, minval=0.9, maxval=0.999)
    a0 = u ** (1.0 / LRU_C)
    lru_lambda = jnp.log(a0) - jnp.log1p(-a0)

    return {
        'x': jax.random.normal(ks[0], (BATCH, SEQ, D_MODEL), f32),
        'positions': jnp.broadcast_to(jnp.arange(SEQ, dtype=jnp.int32), (BATCH, SEQ)),
        'pre_norm_g': gain(ks[1], (DEPTH, D_MODEL)),
        'w_in': w(ks[2], (DEPTH, D_MODEL, D_IN), D_MODEL),
        'conv_dw_w': w(ks[3], (DEPTH, CONV_WIDTH, D_CONV), CONV_WIDTH),
        'conv_dw_b': bias(ks[4], (DEPTH, D_CONV)),
        'conv_ln_g': gain(ks[5], (DEPTH, D_CONV)),
        'conv_ln_b': bias(ks[6], (DEPTH, D_CONV)),
        'w_conv_proj': w(ks[7], (DEPTH, D_CONV, D_MODEL), D_CONV),
        'q_norm_g': gain(ks[8], (DEPTH, Q_LORA)),
        'w_uq': w(ks[9], (DEPTH, Q_LORA, N_HEADS * QK_DIM), Q_LORA),
        'kv_norm_g': gain(ks[10], (DEPTH, KV_LORA)),
        'w_ukv': w(ks[11], (DEPTH, KV_LORA, N_HEADS * (QK_NOPE_DIM + V_DIM)), KV_LORA),
        'w_mla_proj': w(ks[12], (DEPTH, D_MLA, D_MODEL), D_MLA),
        'lru_conv_w': w(ks[13], (DEPTH, LRU_CONV_WIDTH, D_LRU), LRU_CONV_WIDTH),
        'lru_conv_b': bias(ks[14], (DEPTH, D_LRU)),
        'lru_w_a': w(ks[15], (DEPTH, LRU_BLOCKS, LRU_BLOCK_DIM, LRU_BLOCK_DIM), LRU_BLOCK_DIM),
        'lru_b_a': bias(ks[16], (DEPTH, D_LRU)),
        'lru_w_x': w(ks[17], (DEPTH, LRU_BLOCKS, LRU_BLOCK_DIM, LRU_BLOCK_DIM), LRU_BLOCK_DIM),
        'lru_b_x': bias(ks[18], (DEPTH, D_LRU)),
        'lru_lambda': lru_lambda,
        'w_lru_proj': w(ks[19], (DEPTH, D_LRU, D_MODEL), D_LRU),
        'w_out': w(ks[21], (DEPTH, D_MODEL, D_MODEL), D_MODEL),
        'post_norm_g': gain(ks[22], (DEPTH, D_MODEL)),
    }


def reference(x, positions, pre_norm_g, w_in, conv_dw_w, conv_dw_b, conv_ln_g, conv_ln_b,
              w_conv_proj, q_norm_g, w_uq, kv_norm_g, w_ukv, w_mla_proj, lru_conv_w, lru_conv_b,
              lru_w_a, lru_b_a, lru_w_x, lru_b_x, lru_lambda, w_lru_proj, w_out, post_norm_g):
    B, S, _ = x.shape
    inv_freq = ROPE_THETA ** (-jnp.arange(0, QK_ROPE_DIM, 2, dtype=jnp.float32) / QK_ROPE_DIM)
    ang = positions.astype(jnp.float32)[..., None] * inv_freq
    cos = jnp.cos(ang).astype(x.dtype)
    sin = jnp.sin(ang).astype(x.dtype)
    split_at = np.cumsum(IN_SPLITS)[:-1].tolist()

    for l in range(DEPTH):
        h = rms_norm(x, pre_norm_g[l])
        proj = h @ w_in[l]
        (u_glu, g_conv, c_q, c_kv, k_r, g_mla, u_lru, g_lru, g_merge) = jnp.split(proj, split_at, axis=-1)

        y_a = conformer_branch(u_glu, g_conv, conv_dw_w[l], conv_dw_b[l], conv_ln_g[l],
                               conv_ln_b[l], w_conv_proj[l])
        y_b = mla_branch(c_q, c_kv, k_r, g_mla, cos, sin, q_norm_g[l], w_uq[l], kv_norm_g[l],
                         w_ukv[l], w_mla_proj[l])
        y_c = rg_lru_branch(u_lru, g_lru, lru_conv_w[l], lru_conv_b[l], lru_w_a[l], lru_b_a[l],
                            lru_w_x[l], lru_b_x[l], lru_lambda[l], w_lru_proj[l])

        gm = jax.nn.sigmoid(g_merge.reshape(B, S, N_BRANCH, D_MODEL))
        merged = gm[:, :, 0] * y_a + gm[:, :, 1] * y_b + gm[:, :, 2] * y_c
        out = merged @ w_out[l]
        x = x + rms_norm(out, post_norm_g[l])
    return x
```

```python
import contextlib
import math
import numpy as np
import ml_dtypes
import concourse.bass as bass
import concourse.mybir as mybir
from concourse.bass_utils import run_bass_kernel_spmd

F32 = mybir.dt.float32
BF16 = mybir.dt.bfloat16
I32 = mybir.dt.int32
AF = mybir.ActivationFunctionType
ALU = mybir.AluOpType
AX = mybir.AxisListType
NPBF = ml_dtypes.bfloat16

COMPUTE = ("pe", "act", "dve", "pool")
QUEUES = ("sp", "act", "pool")
NDMASEM = 8
EPS = 1e-6
NEG = -30000.0


class Sched:
    def __init__(self, nc):
        self.nc = nc
        self.stack = contextlib.ExitStack()
        self.streams = {e: [] for e in ("pe", "act", "dve", "pool", "sp")}
        self.cnt = {e: 0 for e in COMPUTE}
        self.sem = {e: self.stack.enter_context(nc.semaphore("s_" + e)) for e in COMPUTE}
        self.dsem = {q: [self.stack.enter_context(nc.semaphore("d_%s%d" % (q, i))) for i in range(NDMASEM)]
                     for q in QUEUES}
        self.dcnt = {q: 0 for q in QUEUES}
        self.dtok = {q: [None] * NDMASEM for q in QUEUES}
        self.res = {}
        self.waited = {e: {} for e in self.streams}
        self.scopes = []

    def sbuf(self, name, shape, dt):
        st = self.scopes[-1] if self.scopes else self.stack
        return st.enter_context(self.nc.sbuf_tensor(name, list(shape), dt))

    def psum(self, name, shape, dt):
        st = self.scopes[-1] if self.scopes else self.stack
        return st.enter_context(self.nc.psum_tensor(name, list(shape), dt))

    @contextlib.contextmanager
    def scope(self):
        st = contextlib.ExitStack()
        self.scopes.append(st)
        try:
            yield
        finally:
            self.barrier()
            self.scopes.pop()
            st.close()

    def _need(self, eng, toks, nosame=False):
        best = {}
        for (s, v, e) in toks:
            if nosame and e == eng:
                continue
            if s.name not in best or best[s.name][1] < v:
                best[s.name] = (s, v)
        waits = []
        for key, (s, v) in best.items():
            if self.waited[eng].get(key, 0) >= v:
                continue
            self.waited[eng][key] = v
            waits.append((s, v))
        return waits

    def _deps(self, eng, r, w, nosame=False):
        toks = []
        for k in r:
            st = self.res.get(k)
            if st and st[0] is not None:
                toks.append(st[0])
        for k in w:
            st = self.res.get(k)
            if st:
                if st[0] is not None:
                    toks.append(st[0])
                toks.extend(st[1])
        return self._need(eng, toks, nosame)

    def _commit(self, tok, r, w):
        for k in r:
            self.res.setdefault(k, [None, []])[1].append(tok)
        for k in w:
            self.res[k] = [tok, []]

    def op(self, eng, fn, r=(), w=(), nowait=False, nosame=False):
        waits = [] if nowait else self._deps(eng, r, w, nosame or eng == "pe")
        self.cnt[eng] += 1
        tok = (self.sem[eng], self.cnt[eng], eng)
        self.streams[eng].append((waits, fn, (self.sem[eng], 1)))
        self._commit(tok, r, w)
        return tok

    def dma(self, q, fn, r=(), w=()):
        i = self.dcnt[q]
        self.dcnt[q] += 1
        slot = i % NDMASEM
        s = self.dsem[q][slot]
        val = 16 * (i // NDMASEM + 1)
        waits = self._deps(q, r, w)
        prev = self.dtok[q][slot]
        if prev is not None and self.waited[q].get(s.name, 0) < prev[1]:
            self.waited[q][s.name] = prev[1]
            waits.append((s, prev[1]))
        tok = (s, val, "dma_" + q)
        self.dtok[q][slot] = tok
        self.streams[q].append((waits, fn, (s, 16)))
        self._commit(tok, r, w)
        return tok

    def _all_tokens(self):
        toks = []
        for q in QUEUES:
            for t in self.dtok[q]:
                if t is not None:
                    toks.append(t)
        for e in COMPUTE:
            if self.cnt[e]:
                toks.append((self.sem[e], self.cnt[e], e))
        return toks

    def barrier(self):
        toks = self._all_tokens()
        for e in self.streams:
            waits = self._need(e, toks)
            if waits:
                self.streams[e].append((waits, None, None))

    def emit(self):
        nc = self.nc
        self.barrier()
        streams = self.streams

        def run(name):
            def f(eng):
                for (waits, fn, inc) in streams[name]:
                    for (s, v) in waits:
                        eng.wait_ge(s, v)
                    if fn is not None:
                        fn(eng).then_inc(inc[0], inc[1])
            return f

        with nc.Block() as block:
            block.sync(run("sp"))
            block.tensor(run("pe"))
            block.scalar(run("act"))
            block.vector(run("dve"))
            block.gpsimd(run("pool"))
        self.stack.close()


class KB:
    def __init__(self, nc):
        self.nc = nc
        self.S = Sched(nc)
        self.uid = 0
        self.rr = 0

    def name(self, p):
        self.uid += 1
        return "%s_%d" % (p, self.uid)

    def dram_in(self, name, shape, dt=F32):
        return self.nc.dram_tensor(name, list(shape), dt, kind="ExternalInput").ap()

    def dram_out(self, name, shape, dt=F32):
        return self.nc.dram_tensor(name, list(shape), dt, kind="ExternalOutput").ap()

    def ld(self, out, in_, w, r=(), q="sp"):
        return self.S.dma(q, lambda e: e.dma_start(out=out, in_=in_), r=r, w=w)

    def st(self, out, in_, r, w=(), q="sp"):
        return self.S.dma(q, lambda e: e.dma_start(out=out, in_=in_), r=r, w=w)

    def act(self, out, in_, func, r, w, bias=0.0, scale=1.0):
        return self.S.op("act", lambda e: e.activation(out=out, in_=in_, func=func, bias=bias, scale=scale), r=r, w=w)

    def copy(self, eng, out, in_, r, w):
        if eng == "act":
            return self.act(out, in_, AF.Copy, r, w)
        return self.S.op(eng, lambda e: e.tensor_copy(out=out, in_=in_), r=r, w=w)

    def tt(self, eng, out, a, b, op, r, w):
        return self.S.op(eng, lambda e: e.tensor_tensor(out=out, in0=a, in1=b, op=op), r=r, w=w)

    def ts(self, eng, out, a, s1, op0, r, w, s2=None, op1=None):
        if op1 is None:
            return self.S.op(eng, lambda e: e.tensor_scalar(out=out, in0=a, scalar1=s1, scalar2=None, op0=op0), r=r, w=w)
        return self.S.op(eng, lambda e: e.tensor_scalar(out=out, in0=a, scalar1=s1, scalar2=s2, op0=op0, op1=op1), r=r, w=w)

    def stt(self, out, in0, scalar, in1, op0, op1, r, w):
        return self.S.op("dve", lambda e: e.scalar_tensor_tensor(out=out, in0=in0, scalar=scalar, in1=in1, op0=op0, op1=op1), r=r, w=w)

    def memset(self, eng, ap, val, w):
        return self.S.op(eng, lambda e: e.memset(ap, val), w=w)

    def mm_group(self, out, pairs, r, w):
        n = len(pairs)
        for i, (l, rh) in enumerate(pairs):
            fn = (lambda e, l=l, rh=rh, i=i: e.matmul(out, lhsT=l, rhs=rh, start=(i == 0), stop=(i == n - 1)))
            if i == 0:
                self.S.op("pe", fn, r=r, w=w, nosame=False)
            elif i == n - 1:
                self.S.op("pe", fn, r=r, w=w, nowait=True)
            else:
                self.S.op("pe", fn, nowait=True)

    def rr_eng(self, choices=("act", "dve")):
        self.rr += 1
        return choices[self.rr % len(choices)]


def chunkify(W, M):
    K, n = W.shape
    return np.ascontiguousarray(W.reshape(K // 128, 128, n // M, M).transpose(2, 1, 0, 3))


def pvec(v):
    return np.ascontiguousarray(v.reshape(-1, 128).T)


class WStream:
    def __init__(self, kb, KCmax, Mmax, nbuf=2, tag="w"):
        self.kb = kb
        self.n = nbuf
        self.i = 0
        self.tag = kb.name(tag)
        self.st = [kb.S.sbuf("%s_st%d" % (self.tag, i), [128, KCmax, Mmax], F32) for i in range(nbuf)]
        self.bf = [kb.S.sbuf("%s_bf%d" % (self.tag, i), [128, KCmax, Mmax], BF16) for i in range(nbuf)]

    def get(self, src, KC, M):
        kb = self.kb
        s = self.i % self.n
        self.i += 1
        ks, kbk = "%s_st%d" % (self.tag, s), "%s_bf%d" % (self.tag, s)
        kb.ld(self.st[s][:, :KC, :M], src, w=[ks])
        kb.copy(kb.rr_eng(), self.bf[s][:, :KC, :M], self.st[s][:, :KC, :M], r=[ks], w=[kbk])
        return self.bf[s], kbk


def make_hT(kb, cfg, xT, t0, TP, hT, hkey, g_sb, ones, ps_ss, x3, tmp):
    D = cfg["D"]
    KC = D // 128
    sq = tmp["sq"]
    for kc in range(KC):
        b = kc % 3
        kb.ld(x3[b][:, :TP], xT[kc * 128:(kc + 1) * 128, t0:t0 + TP], w=["x3_%d" % b])
        kb.act(sq[kc % 2][:, :TP], x3[b][:, :TP], AF.Square, r=["x3_%d" % b], w=["sq%d" % (kc % 2)])
        kb.S.op("pe", lambda e, kc=kc: e.matmul(ps_ss[0][:, :TP], lhsT=ones[:], rhs=sq[kc % 2][:, :TP],
                                                 start=(kc == 0), stop=(kc == KC - 1)),
                r=["sq%d" % (kc % 2), "ones"], w=[ps_ss[1]])
    rstd = tmp["rstd"]
    kb.act(rstd[:, :TP], ps_ss[0][:, :TP], AF.Sqrt, r=[ps_ss[1], "epsc"], w=["rstd"], bias=tmp["eps"][:, 0:1], scale=1.0 / D)
    kb.S.op("dve", lambda e: e.reciprocal(out=rstd[:, :TP], in_=rstd[:, :TP]), r=["rstd"], w=["rstd"])
    for kc in range(KC):
        b = kc % 3
        kb.ld(x3[b][:, :TP], xT[kc * 128:(kc + 1) * 128, t0:t0 + TP], w=["x3_%d" % b])
        kb.stt(hT[:, kc, :TP], x3[b][:, :TP], g_sb[:, kc:kc + 1], rstd[:, :TP], ALU.mult, ALU.mult,
               r=["x3_%d" % b, "rstd", "gpre"], w=[hkey])


def rope_tables(kb, cfg, pos_dram, t0, n, cos, sin, invf, tmpi, tmpf, key, P=32):
    PI, TWO_PI = math.pi, 2.0 * math.pi
    C1 = 6.28125
    C2 = TWO_PI - C1
    A, Kf, M = tmpf[0], tmpf[1], tmpf[2]
    kA, kK, kM = key + "_A", key + "_K", key + "_M"
    kb.ld(tmpi[:P, :n], pos_dram[0:1, t0:t0 + n].partition_broadcast(P), w=[key + "_pi"])
    kb.copy("dve", A[:P, :n], tmpi[:P, :n], r=[key + "_pi"], w=[kA])
    kb.ts("dve", A[:P, :n], A[:P, :n], invf[:P, 0:1], ALU.mult, r=[kA, "invf"], w=[kA])
    kb.ts("dve", Kf[:P, :n], A[:P, :n], 1.0 / TWO_PI, ALU.mult, r=[kA], w=[kK])
    kb.copy("dve", tmpi[:P, :n], Kf[:P, :n], r=[kK], w=[key + "_pi"])
    kb.copy("dve", Kf[:P, :n], tmpi[:P, :n], r=[key + "_pi"], w=[kK])
    kb.stt(A[:P, :n], Kf[:P, :n], -C1, A[:P, :n], ALU.mult, ALU.add, r=[kK, kA], w=[kA])
    kb.stt(A[:P, :n], Kf[:P, :n], -C2, A[:P, :n], ALU.mult, ALU.add, r=[kK, kA], w=[kA])

    def wrap(T_, kT):
        kb.ts("dve", M[:P, :n], T_[:P, :n], PI, ALU.is_gt, r=[kT], w=[kM])
        kb.stt(T_[:P, :n], M[:P, :n], -TWO_PI, T_[:P, :n], ALU.mult, ALU.add, r=[kM, kT], w=[kT])
        kb.ts("dve", M[:P, :n], T_[:P, :n], -PI, ALU.is_lt, r=[kT], w=[kM])
        kb.stt(T_[:P, :n], M[:P, :n], TWO_PI, T_[:P, :n], ALU.mult, ALU.add, r=[kM, kT], w=[kT])
        kb.ts("dve", T_[:P, :n], T_[:P, :n], -3.1415925, ALU.max, r=[kT], w=[kT], s2=3.1415925, op1=ALU.min)

    wrap(A, kA)
    kb.act(sin[:P, :n], A[:P, :n], AF.Sin, r=[kA], w=[key + "_s"])
    kb.ts("dve", A[:P, :n], A[:P, :n], PI / 2.0, ALU.add, r=[kA], w=[kA])
    wrap(A, kA)
    kb.act(cos[:P, :n], A[:P, :n], AF.Sin, r=[kA], w=[key + "_c"])


def load_consts(kb, ones_d, invf_d):
    S = kb.S
    kb.ones = S.sbuf("ones", [128, 128], F32)
    kb.ld(kb.ones[:], ones_d, w=["ones"])
    kb.invf = S.sbuf("invf", [32, 1], F32)
    kb.ld(kb.invf[:], invf_d, w=["invf"])
    kb.negpi = S.sbuf("negpi", [128, 1], F32)
    kb.memset("pool", kb.negpi[:], -math.pi, w=["negpi"])
    kb.epsc = S.sbuf("epsc", [128, 1], F32)
    kb.memset("pool", kb.epsc[:], EPS, w=["epsc"])


def phase_A(kb, cfg, d):
    S = kb.S
    D, T, TP = cfg["D"], cfg["T"], cfg["TP"]
    DC, QL, KVL, DL = cfg["DC"], cfg["QL"], cfg["KVL"], cfg["DL"]
    KC = D // 128
    with S.scope():
        hT = S.sbuf("A_hT", [128, KC, TP], BF16)
        x3 = [S.sbuf("A_x3_%d" % i, [128, TP], F32) for i in range(3)]
        tmp = {"sq": [S.sbuf("A_sq%d" % i, [128, TP], F32) for i in range(2)],
               "rstd": S.sbuf("A_rstd", [128, TP], F32), "eps": kb.epsc}
        gpre = S.sbuf("A_gpre", [128, KC], F32)
        kb.ld(gpre[:], d["gpre"], w=["gpre"])
        gq = S.sbuf("A_gq", [128, QL // 128], F32)
        kb.ld(gq[:], d["gq"], w=["gq"])
        gkv = S.sbuf("A_gkv", [128, KVL // 128], F32)
        kb.ld(gkv[:], d["gkv"], w=["gkv"])
        ws = WStream(kb, KC, 128, nbuf=2, tag="Aw")
        ps = [S.psum("A_ps%d" % i, [128, 512], F32) for i in range(6)]
        cq = S.sbuf("A_cq", [128, QL // 128, TP], F32)
        ckv = S.sbuf("A_ckv", [128, KVL // 128, TP], F32)
        rq = S.sbuf("A_rq", [128, TP], F32)
        o32 = [S.sbuf("A_o32_%d" % i, [128, TP], F32) for i in range(3)]
        ob = [S.sbuf("A_ob_%d" % i, [128, TP], BF16) for i in range(2)]
        sg = S.sbuf("A_sg", [128, TP], F32)
        cos = S.sbuf("A_cos", [32, TP], F32)
        sin = S.sbuf("A_sin", [32, TP], F32)
        tmpi = S.sbuf("A_tmpi", [32, TP], I32)
        tmpf = [S.sbuf("A_tmpf%d" % i, [32, TP], F32) for i in range(3)]
        oi = [0, 0]

        def proj(src, M, pi):
            wb, wk = ws.get(src, KC, M)
            kb.mm_group(ps[pi][:M, :TP], [(wb[:, kc, :M], hT[:, kc, :TP]) for kc in range(KC)],
                        r=[wk, "A_hT"], w=["A_ps%d" % pi])

        def next_o32():
            oi[0] += 1
            return o32[oi[0] % 3], "A_o32_%d" % (oi[0] % 3)

        def next_ob():
            oi[1] += 1
            return ob[oi[1] % 2], "A_ob_%d" % (oi[1] % 2)

        for t0 in range(0, T, TP):
            make_hT(kb, cfg, d["xT"], t0, TP, hT, "A_hT", gpre, kb.ones, (ps[5], "A_ps5"), x3, tmp)
            parts = cfg.get("parts", "glqr")
            for j in range(DC // 128 if "g" in parts else 0):
                proj(d["wa_a"][j], 128, 0)
                proj(d["wa_b"][j], 128, 1)
                kb.act(sg[:, :TP], ps[1][:, :TP], AF.Sigmoid, r=["A_ps1"], w=["A_sg"])
                o, ok = next_o32()
                kb.tt("dve", o[:, :TP], ps[0][:, :TP], sg[:, :TP], ALU.mult, r=["A_ps0", "A_sg"], w=[ok])
                kb.st(d["gT"][j * 128:(j + 1) * 128, t0:t0 + TP], o[:, :TP], r=[ok])
            for j in range(DL // 128 if "l" in parts else 0):
                proj(d["wa_u"][j], 128, j % 2)
                o, ok = next_o32()
                kb.copy(kb.rr_eng(), o[:, :TP], ps[j % 2][:, :TP], r=["A_ps%d" % (j % 2)], w=[ok])
                kb.st(d["uT"][j * 128:(j + 1) * 128, t0:t0 + TP], o[:, :TP], r=[ok])
            for (wname, n, buf, bkey, gv, gk, outname) in (("wa_cq", QL, cq, "A_cq", gq, "gq", "cqnT"),
                                                            ("wa_ckv", KVL, ckv, "A_ckv", gkv, "gkv", "ckvnT")):
                if "q" not in parts:
                    break
                nch = n // 128
                for j in range(nch):
                    proj(d[wname][j], 128, j % 2)
                    kb.copy(kb.rr_eng(), buf[:, j, :TP], ps[j % 2][:, :TP], r=["A_ps%d" % (j % 2)], w=[bkey + str(j)])
                for j in range(nch):
                    kb.act(tmp["sq"][j % 2][:, :TP], buf[:, j, :TP], AF.Square, r=[bkey + str(j)], w=["sq%d" % (j % 2)])
                    kb.S.op("pe", lambda e, j=j, nch=nch: e.matmul(ps[4][:, :TP], lhsT=kb.ones[:], rhs=tmp["sq"][j % 2][:, :TP],
                                                                  start=(j == 0), stop=(j == nch - 1)),
                            r=["sq%d" % (j % 2), "ones"], w=["A_ps4"])
                kb.act(rq[:, :TP], ps[4][:, :TP], AF.Sqrt, r=["A_ps4", "epsc"], w=["A_rq"], bias=kb.epsc[:, 0:1], scale=1.0 / n)
                kb.S.op("dve", lambda e: e.reciprocal(out=rq[:, :TP], in_=rq[:, :TP]), r=["A_rq"], w=["A_rq"])
                for j in range(nch):
                    o, ok = next_ob()
                    kb.stt(o[:, :TP], buf[:, j, :TP], gv[:, j:j + 1], rq[:, :TP], ALU.mult, ALU.mult,
                           r=[bkey + str(j), "A_rq", gk], w=[ok])
                    kb.st(d[outname][j * 128:(j + 1) * 128, t0:t0 + TP], o[:, :TP], r=[ok])
            if "r" not in parts:
                continue
            rope_tables(kb, cfg, d["pos"], t0, TP, cos, sin, kb.invf, tmpi, tmpf, "A_rt")
            proj(d["wa_kr"][0], 32, 2)
            proj(d["wa_kr"][1], 32, 3)
            x1, x2 = ps[2], ps[3]
            ta, tak = next_o32()
            tb, tbk = next_o32()
            kb.tt("dve", ta[:32, :TP], x1[:32, :TP], cos[:32, :TP], ALU.mult, r=["A_ps2", "A_rt_c"], w=[tak])
            kb.tt("dve", tb[:32, :TP], x2[:32, :TP], sin[:32, :TP], ALU.mult, r=["A_ps3", "A_rt_s"], w=[tbk])
            o, ok = next_ob()
            kb.tt("dve", o[:32, :TP], ta[:32, :TP], tb[:32, :TP], ALU.subtract, r=[tak, tbk], w=[ok])
            kb.st(d["kr1T"][:, t0:t0 + TP], o[:32, :TP], r=[ok])
            ta, tak = next_o32()
            tb, tbk = next_o32()
            kb.tt("dve", ta[:32, :TP], x1[:32, :TP], sin[:32, :TP], ALU.mult, r=["A_ps2", "A_rt_s"], w=[tak])
            kb.tt("dve", tb[:32, :TP], x2[:32, :TP], cos[:32, :TP], ALU.mult, r=["A_ps3", "A_rt_c"], w=[tbk])
            o, ok = next_ob()
            kb.tt("dve", o[:32, :TP], ta[:32, :TP], tb[:32, :TP], ALU.add, r=[tak, tbk], w=[ok])
            kb.st(d["kr2T"][:, t0:t0 + TP], o[:32, :TP], r=[ok])


def build_A(cfg):
    nc = bass.Bass("TRN2", target_bir_lowering=False)
    kb = KB(nc)
    D, T = cfg["D"], cfg["T"]
    DC, QL, KVL, DL = cfg["DC"], cfg["QL"], cfg["KVL"], cfg["DL"]
    KC = D // 128
    d = {
        "xT": kb.dram_in("xT", [D, T]), "pos": kb.dram_in("pos", [1, T], I32),
        "gpre": kb.dram_in("gpre", [128, KC]), "gq": kb.dram_in("gq", [128, QL // 128]),
        "gkv": kb.dram_in("gkv", [128, KVL // 128]),
        "wa_a": kb.dram_in("wa_a", [DC // 128, 128, KC, 128]), "wa_b": kb.dram_in("wa_b", [DC // 128, 128, KC, 128]),
        "wa_cq": kb.dram_in("wa_cq", [QL // 128, 128, KC, 128]), "wa_ckv": kb.dram_in("wa_ckv", [KVL // 128, 128, KC, 128]),
        "wa_kr": kb.dram_in("wa_kr", [2, 128, KC, 32]), "wa_u": kb.dram_in("wa_u", [DL // 128, 128, KC, 128]),
        "gT": kb.dram_out("gT", [DC, T]), "uT": kb.dram_out("uT", [DL, T]),
        "cqnT": kb.dram_out("cqnT", [QL, T], BF16), "ckvnT": kb.dram_out("ckvnT", [KVL, T], BF16),
        "kr1T": kb.dram_out("kr1T", [32, T], BF16), "kr2T": kb.dram_out("kr2T", [32, T], BF16),
    }
    load_consts(kb, kb.dram_in("c_ones", [128, 128]), kb.dram_in("c_invf", [32, 1]))
    phase_A(kb, cfg, d)
    kb.S.emit()
    return nc


def col_splits(cfg):
    D, DC, QL, KVL, H, DL = cfg["D"], cfg["DC"], cfg["QL"], cfg["KVL"], cfg["H"], cfg["DL"]
    sizes = (("a", DC), ("b", DC), ("gconv", DC), ("cq", QL), ("ckv", KVL), ("kr", 64), ("gmla", H * 128),
             ("u", DL), ("glru", DL), ("gm0", D), ("gm1", D), ("gm2", D))
    off, o = {}, 0
    for n, s in sizes:
        off[n] = (o, o + s)
        o += s
    return off


def consts():
    invf = (10000.0 ** (-np.arange(0, 64, 2, dtype=np.float32) / 64)).astype(np.float32).reshape(32, 1)
    return {"c_ones": np.ones((128, 128), np.float32), "c_invf": invf}


def prep_A(cfg, w_in, pre_g, q_g, kv_g):
    cs = col_splits(cfg)
    sl = lambda n: w_in[:, cs[n][0]:cs[n][1]]
    d = {"gpre": pvec(pre_g), "gq": pvec(q_g), "gkv": pvec(kv_g),
         "wa_a": chunkify(sl("a"), 128), "wa_b": chunkify(sl("b"), 128), "wa_cq": chunkify(sl("cq"), 128),
         "wa_ckv": chunkify(sl("ckv"), 128), "wa_kr": chunkify(sl("kr"), 32), "wa_u": chunkify(sl("u"), 128)}
    d.update(consts())
    return d


def run(nc, in_maps, ncores):
    res = run_bass_kernel_spmd(nc, in_maps, core_ids=list(range(ncores)))
    return res.results


def phase_C(kb, cfg, d):
    S = kb.S
    D, T, TP = cfg["D"], cfg["T"], cfg["TPC"]
    DC, DL, HV = cfg["DC"], cfg["DL"], cfg["H"] * 128
    KC = D // 128
    NA, NB, NL = DC // 128, HV // 128, DL // 128
    with S.scope():
        hT = S.sbuf("C_hT", [128, KC, TP], BF16)
        zA = S.sbuf("C_zA", [128, NA, TP], BF16)
        zB = S.sbuf("C_zB", [128, NB, TP], BF16)
        zC = S.sbuf("C_zC", [128, NL, TP], BF16)
        mT = S.sbuf("C_mT", [128, KC, TP], BF16)
        x3 = [S.sbuf("C_x3_%d" % i, [128, TP], F32) for i in range(3)]
        tmp = {"sq": [S.sbuf("C_sq%d" % i, [128, TP], F32) for i in range(2)],
               "rstd": S.sbuf("C_rstd", [128, TP], F32), "eps": kb.epsc}
        gpre = S.sbuf("C_gpre", [128, KC], F32)
        kb.ld(gpre[:], d["gpre"], w=["gpre"])
        gpost = S.sbuf("C_gpost", [128, KC], F32)
        kb.ld(gpost[:], d["gpost"], w=["gpost"])
        lng = S.sbuf("C_lng", [128, NA], F32)
        kb.ld(lng[:], d["lng"], w=["lng"])
        lnb = S.sbuf("C_lnb", [128, NA], F32)
        kb.ld(lnb[:], d["lnb"], w=["lnb"])
        ws = WStream(kb, KC, 128, nbuf=2, tag="Cw")
        ps = [S.psum("C_ps%d" % i, [128, 512], F32) for i in range(7)]
        f = {n: S.sbuf("C_" + n, [128, TP], F32) for n in
             ("acc1", "acc2", "mean", "lrstd", "t1", "t2", "t3", "sg0", "sg1", "sg2", "in0", "in1", "prstd", "o0", "o1")}

        def proj(src, nk, rhsbuf, rkey, pi):
            wb, wk = ws.get(src, nk, 128)
            kb.mm_group(ps[pi][:, :TP], [(wb[:, kc, :], rhsbuf[:, kc, :TP]) for kc in range(nk)],
                        r=[wk, rkey], w=["C_ps%d" % pi])

        def onesum(src, skey, pi):
            kb.S.op("pe", lambda e: e.matmul(ps[pi][:, :TP], lhsT=kb.ones[:], rhs=src[:, :TP], start=True, stop=True),
                    r=[skey, "ones"], w=["C_ps%d" % pi])

        for t0 in range(0, T, TP):
            make_hT(kb, cfg, d["xT"], t0, TP, hT, "C_hT", gpre, kb.ones, (ps[6], "C_ps6"), x3, tmp)
            for j in range(NA):
                b = j % 2
                kb.ld(f["in%d" % b][:, :TP], d["yconvT"][j * 128:(j + 1) * 128, t0:t0 + TP], w=["C_in%d" % b])
                if j == 0:
                    kb.copy("dve", f["acc1"][:, :TP], f["in%d" % b][:, :TP], r=["C_in%d" % b], w=["C_acc1"])
                    kb.tt("pool", f["acc2"][:, :TP], f["in%d" % b][:, :TP], f["in%d" % b][:, :TP], ALU.mult, r=["C_in%d" % b], w=["C_acc2"])
                else:
                    kb.tt("dve", f["acc1"][:, :TP], f["acc1"][:, :TP], f["in%d" % b][:, :TP], ALU.add, r=["C_in%d" % b, "C_acc1"], w=["C_acc1"])
                    kb.tt("pool", f["t1"][:, :TP], f["in%d" % b][:, :TP], f["in%d" % b][:, :TP], ALU.mult, r=["C_in%d" % b], w=["C_t1"])
                    kb.tt("pool", f["acc2"][:, :TP], f["acc2"][:, :TP], f["t1"][:, :TP], ALU.add, r=["C_t1", "C_acc2"], w=["C_acc2"])
            onesum(f["acc1"], "C_acc1", 4)
            onesum(f["acc2"], "C_acc2", 5)
            kb.ts("dve", f["mean"][:, :TP], ps[4][:, :TP], 1.0 / DC, ALU.mult, r=["C_ps4"], w=["C_mean"])
            kb.tt("dve", f["t1"][:, :TP], f["mean"][:, :TP], f["mean"][:, :TP], ALU.mult, r=["C_mean"], w=["C_t1"])
            kb.stt(f["t2"][:, :TP], ps[5][:, :TP], 1.0 / DC, f["t1"][:, :TP], ALU.mult, ALU.subtract, r=["C_ps5", "C_t1"], w=["C_t2"])
            kb.act(f["lrstd"][:, :TP], f["t2"][:, :TP], AF.Sqrt, r=["C_t2", "epsc"], w=["C_lrstd"], bias=kb.epsc[:, 0:1], scale=1.0)
            kb.S.op("dve", lambda e: e.reciprocal(out=f["lrstd"][:, :TP], in_=f["lrstd"][:, :TP]), r=["C_lrstd"], w=["C_lrstd"])
            for j in range(NA):
                b = j % 2
                proj(d["wc_gconv"][j], KC, hT, "C_hT", b)
                kb.act(f["sg%d" % b][:, :TP], ps[b][:, :TP], AF.Silu, r=["C_ps%d" % b], w=["C_sg%d" % b])
                kb.ld(f["in%d" % b][:, :TP], d["yconvT"][j * 128:(j + 1) * 128, t0:t0 + TP], w=["C_in%d" % b])
                kb.tt("pool", f["t1"][:, :TP], f["in%d" % b][:, :TP], f["mean"][:, :TP], ALU.subtract, r=["C_in%d" % b, "C_mean"], w=["C_t1"])
                kb.tt("pool", f["t1"][:, :TP], f["t1"][:, :TP], f["lrstd"][:, :TP], ALU.mult, r=["C_t1", "C_lrstd"], w=["C_t1"])
                kb.act(f["t2"][:, :TP], f["t1"][:, :TP], AF.Silu, r=["C_t1", "lng", "lnb"], w=["C_t2"],
                       bias=lnb[:, j:j + 1], scale=lng[:, j:j + 1])
                kb.tt("dve", zA[:, j, :TP], f["t2"][:, :TP], f["sg%d" % b][:, :TP], ALU.mult, r=["C_t2", "C_sg%d" % b], w=["C_zA"])
            for (wname, src, n, zbuf, zkey) in (("wc_gmla", "oT", NB, zB, "C_zB"), ("wc_glru", "hlruT", NL, zC, "C_zC")):
                for j in range(n):
                    b = j % 2
                    proj(d[wname][j], KC, hT, "C_hT", b)
                    kb.act(f["sg%d" % b][:, :TP], ps[b][:, :TP], AF.Silu, r=["C_ps%d" % b], w=["C_sg%d" % b])
                    kb.ld(f["in%d" % b][:, :TP], d[src][j * 128:(j + 1) * 128, t0:t0 + TP], w=["C_in%d" % b])
                    kb.tt("dve" if j % 2 else "pool", zbuf[:, j, :TP], f["in%d" % b][:, :TP], f["sg%d" % b][:, :TP], ALU.mult,
                          r=["C_in%d" % b, "C_sg%d" % b], w=[zkey])
            for m in range(KC):
                proj(d["wp_conv"][m], NA, zA, "C_zA", 0)
                proj(d["wp_mla"][m], NB, zB, "C_zB", 1)
                proj(d["wp_lru"][m], NL, zC, "C_zC", 2)
                for i in range(3):
                    proj(d["wc_gm"][i * KC + m], KC, hT, "C_hT", 3)
                    kb.act(f["sg%d" % i][:, :TP], ps[3][:, :TP], AF.Sigmoid, r=["C_ps3"], w=["C_sg%d" % i])
                kb.tt("dve", f["t1"][:, :TP], ps[0][:, :TP], f["sg0"][:, :TP], ALU.mult, r=["C_ps0", "C_sg0"], w=["C_t1"])
                kb.tt("dve", f["t2"][:, :TP], ps[1][:, :TP], f["sg1"][:, :TP], ALU.mult, r=["C_ps1", "C_sg1"], w=["C_t2"])
                kb.tt("dve", f["t3"][:, :TP], ps[2][:, :TP], f["sg2"][:, :TP], ALU.mult, r=["C_ps2", "C_sg2"], w=["C_t3"])
                kb.tt("pool", f["t1"][:, :TP], f["t1"][:, :TP], f["t2"][:, :TP], ALU.add, r=["C_t1", "C_t2"], w=["C_t1"])
                kb.tt("pool", mT[:, m, :TP], f["t1"][:, :TP], f["t3"][:, :TP], ALU.add, r=["C_t1", "C_t3"], w=["C_mT"])
            for m in range(KC):
                proj(d["wout"][m], KC, mT, "C_mT", m % 2)
                if m == 0:
                    kb.act(f["acc1"][:, :TP], ps[0][:, :TP], AF.Square, r=["C_ps0"], w=["C_acc1"])
                else:
                    kb.act(f["t1"][:, :TP], ps[m % 2][:, :TP], AF.Square, r=["C_ps%d" % (m % 2)], w=["C_t1"])
                    kb.tt("dve", f["acc1"][:, :TP], f["acc1"][:, :TP], f["t1"][:, :TP], ALU.add, r=["C_t1", "C_acc1"], w=["C_acc1"])
            onesum(f["acc1"], "C_acc1", 4)
            kb.act(f["prstd"][:, :TP], ps[4][:, :TP], AF.Sqrt, r=["C_ps4", "epsc"], w=["C_prstd"], bias=kb.epsc[:, 0:1], scale=1.0 / D)
            kb.S.op("dve", lambda e: e.reciprocal(out=f["prstd"][:, :TP], in_=f["prstd"][:, :TP]), r=["C_prstd"], w=["C_prstd"])
            for m in range(KC):
                b = m % 2
                proj(d["wout"][m], KC, mT, "C_mT", b)
                kb.ld(f["in%d" % b][:, :TP], d["xT"][m * 128:(m + 1) * 128, t0:t0 + TP], w=["C_in%d" % b])
                kb.stt(f["t%d" % (b + 1)][:, :TP], ps[b][:, :TP], gpost[:, m:m + 1], f["prstd"][:, :TP], ALU.mult, ALU.mult,
                       r=["C_ps%d" % b, "gpost", "C_prstd"], w=["C_t%d" % (b + 1)])
                kb.tt("pool", f["o%d" % b][:, :TP], f["t%d" % (b + 1)][:, :TP], f["in%d" % b][:, :TP], ALU.add,
                      r=["C_t%d" % (b + 1), "C_in%d" % b], w=["C_o%d" % b])
                kb.st(d["xTn"][m * 128:(m + 1) * 128, t0:t0 + TP], f["o%d" % b][:, :TP], r=["C_o%d" % b])


def build_C(cfg):
    nc = bass.Bass("TRN2", target_bir_lowering=False)
    kb = KB(nc)
    D, T = cfg["D"], cfg["T"]
    DC, DL, HV = cfg["DC"], cfg["DL"], cfg["H"] * 128
    KC = D // 128
    d = {
        "xT": kb.dram_in("xT", [D, T]), "gpre": kb.dram_in("gpre", [128, KC]), "gpost": kb.dram_in("gpost", [128, KC]),
        "lng": kb.dram_in("lng", [128, DC // 128]), "lnb": kb.dram_in("lnb", [128, DC // 128]),
        "yconvT": kb.dram_in("yconvT", [DC, T]), "oT": kb.dram_in("oT", [HV, T]), "hlruT": kb.dram_in("hlruT", [DL, T]),
        "wc_gconv": kb.dram_in("wc_gconv", [DC // 128, 128, KC, 128]), "wc_gmla": kb.dram_in("wc_gmla", [HV // 128, 128, KC, 128]),
        "wc_glru": kb.dram_in("wc_glru", [DL // 128, 128, KC, 128]), "wc_gm": kb.dram_in("wc_gm", [3 * KC, 128, KC, 128]),
        "wp_conv": kb.dram_in("wp_conv", [KC, 128, DC // 128, 128]), "wp_mla": kb.dram_in("wp_mla", [KC, 128, HV // 128, 128]),
        "wp_lru": kb.dram_in("wp_lru", [KC, 128, DL // 128, 128]), "wout": kb.dram_in("wout", [KC, 128, KC, 128]),
        "xTn": kb.dram_out("xTn", [D, T]),
    }
    load_consts(kb, kb.dram_in("c_ones", [128, 128]), kb.dram_in("c_invf", [32, 1]))
    phase_C(kb, cfg, d)
    kb.S.emit()
    return nc


def prep_C(cfg, p):
    cs = col_splits(cfg)
    w_in = p["w_in"]
    sl = lambda n: w_in[:, cs[n][0]:cs[n][1]]
    d = {"gpre": pvec(p["pre_norm_g"]), "gpost": pvec(p["post_norm_g"]), "lng": pvec(p["conv_ln_g"]), "lnb": pvec(p["conv_ln_b"]),
         "wc_gconv": chunkify(sl("gconv"), 128), "wc_gmla": chunkify(sl("gmla"), 128), "wc_glru": chunkify(sl("glru"), 128),
         "wc_gm": np.concatenate([chunkify(sl("gm0"), 128), chunkify(sl("gm1"), 128), chunkify(sl("gm2"), 128)], 0),
         "wp_conv": chunkify(p["w_conv_proj"], 128), "wp_mla": chunkify(p["w_mla_proj"], 128),
         "wp_lru": chunkify(p["w_lru_proj"], 128), "wout": chunkify(p["w_out"], 128)}
    d.update(consts())
    return d


def pe_seq(kb, fns, r, w):
    n = len(fns)
    for i, fn in enumerate(fns):
        if i == 0:
            kb.S.op("pe", fn, r=r, w=w)
        elif i == n - 1:
            kb.S.op("pe", fn, r=r, w=w, nowait=True)
        else:
            kb.S.op("pe", fn, nowait=True)


def phase_B_conv(kb, cfg, d):
    S = kb.S
    SEQ, CPC = cfg["S"], cfg["DC"] // cfg["NC"]
    PIECE = min(2048, SEQ)
    nch = CPC // 128
    with S.scope():
        gin = S.sbuf("B_gin", [128, 30 + SEQ], F32)
        acc = S.sbuf("B_acc", [128, SEQ], F32)
        cw = S.sbuf("B_cw", [128, nch, 31], F32)
        cb = S.sbuf("B_cb", [128, nch], F32)
        kb.ld(cw[:], d["convw"], w=["B_cw"])
        kb.ld(cb[:], d["convb"], w=["B_cb"])
        kb.memset("pool", gin[:, 0:30], 0.0, w=["B_gin"])
        for cc in range(nch):
            kb.ld(gin[:, 30:30 + SEQ], d["gTc"][cc * 128:(cc + 1) * 128, :], w=["B_gin"])
            for p0 in range(0, SEQ, PIECE):
                ak = "B_acc%d" % p0
                a = acc[:, p0:p0 + PIECE]
                kb.ts("dve", a, gin[:, p0:p0 + PIECE], cw[:, cc, 0:1], ALU.mult, r=["B_gin", "B_cw"], w=[ak])
                for k in range(1, 31):
                    kb.stt(a, gin[:, p0 + k:p0 + k + PIECE], cw[:, cc, k:k + 1], a, ALU.mult, ALU.add, r=["B_gin", "B_cw", ak], w=[ak])
                kb.ts("dve", a, a, cb[:, cc:cc + 1], ALU.add, r=[ak, "B_cb"], w=[ak])
                kb.st(d["yconvTc"][cc * 128:(cc + 1) * 128, p0:p0 + PIECE], a, r=[ak])


def phase_B_lru(kb, cfg, d):
    S = kb.S
    SEQ, LPC = cfg["S"], cfg["DL"] // cfg["NC"]
    PIECE = min(2048, SEQ)
    nb = LPC // 128
    BLK = 512
    with S.scope():
        uin = S.sbuf("L_uin", [128, 3 + SEQ], F32)
        xc = S.sbuf("L_xc", [128, SEQ], F32)
        xcb = S.sbuf("L_xcb", [128, SEQ], BF16)
        hb = S.sbuf("L_h", [128, SEQ], F32)
        lw = S.sbuf("L_lw", [128, nb, 4], F32)
        kb.ld(lw[:], d["lruw"], w=["L_lw"])
        vec = {}
        for n in ("lrub", "ba", "bx", "lam"):
            vec[n] = S.sbuf("L_" + n, [128, nb], F32)
            kb.ld(vec[n][:], d[n], w=["L_" + n])
        onec = S.sbuf("L_onec", [128, 1], F32)
        kb.memset("pool", onec[:], 1.0, w=["L_onec"])
        cc = S.sbuf("L_c", [128, 2], F32)
        wst = [S.sbuf("L_wst%d" % i, [128, 128], F32) for i in range(2)]
        wbf = [S.sbuf("L_wbf%d" % i, [128, 128], BF16) for i in range(2)]
        ps = [S.psum("L_ps%d" % i, [128, 512], F32) for i in range(2)]
        f = {n: S.sbuf("L_" + n, [128, BLK], F32) for n in ("r", "i", "a", "m", "b")}
        kb.memset("pool", uin[:, 0:3], 0.0, w=["L_uin"])
        for bi in range(nb):
            kb.ld(uin[:, 3:3 + SEQ], d["uTc"][bi * 128:(bi + 1) * 128, :], w=["L_uin"])
            for i, nm in enumerate(("lwa", "lwx")):
                kb.ld(wst[i][:], d[nm][bi], w=["L_wst%d" % i])
                kb.copy("act", wbf[i][:], wst[i][:], r=["L_wst%d" % i], w=["L_wbf%d" % i])
            kb.act(cc[:, 0:1], vec["lam"][:, bi:bi + 1], AF.Exp, r=["L_lam"], w=["L_c"], scale=-1.0)
            kb.act(cc[:, 0:1], cc[:, 0:1], AF.Ln, r=["L_c", "L_onec"], w=["L_c"], bias=onec[:, 0:1])
            kb.ts("dve", cc[:, 1:2], cc[:, 0:1], -16.0, ALU.mult, r=["L_c"], w=["L_c2"])
            kb.ts("dve", cc[:, 0:1], cc[:, 0:1], -8.0, ALU.mult, r=["L_c", "L_c2"], w=["L_c"])
            for p0 in range(0, SEQ, PIECE):
                a = xc[:, p0:p0 + PIECE]
                kb.ts("dve", a, uin[:, p0:p0 + PIECE], lw[:, bi, 0:1], ALU.mult, r=["L_uin", "L_lw"], w=["L_xc"])
                for k in range(1, 4):
                    kb.stt(a, uin[:, p0 + k:p0 + k + PIECE], lw[:, bi, k:k + 1], a, ALU.mult, ALU.add, r=["L_uin", "L_lw", "L_xc"], w=["L_xc"])
                kb.ts("dve", a, a, vec["lrub"][:, bi:bi + 1], ALU.add, r=["L_xc", "L_lrub"], w=["L_xc"])
                kb.copy("act", xcb[:, p0:p0 + PIECE], a, r=["L_xc"], w=["L_xcb"])
            for blk in range(SEQ // BLK):
                t0 = blk * BLK
                kb.mm_group(ps[0][:, :BLK], [(wbf[0][:], xcb[:, t0:t0 + BLK])], r=["L_wbf0", "L_xcb"], w=["L_ps0"])
                kb.mm_group(ps[1][:, :BLK], [(wbf[1][:], xcb[:, t0:t0 + BLK])], r=["L_wbf1", "L_xcb"], w=["L_ps1"])
                kb.act(f["r"][:], ps[0][:, :BLK], AF.Sigmoid, r=["L_ps0", "L_ba"], w=["L_r"], bias=vec["ba"][:, bi:bi + 1])
                kb.act(f["i"][:], ps[1][:, :BLK], AF.Sigmoid, r=["L_ps1", "L_bx"], w=["L_i"], bias=vec["bx"][:, bi:bi + 1])
                kb.act(f["a"][:], f["r"][:], AF.Exp, r=["L_r", "L_c"], w=["L_a"], scale=cc[:, 0:1])
                kb.act(f["m"][:], f["r"][:], AF.Exp, r=["L_r", "L_c2"], w=["L_m"], scale=cc[:, 1:2])
                kb.ts("dve", f["m"][:], f["m"][:], -1.0, ALU.mult, r=["L_m"], w=["L_m"], s2=1.0, op1=ALU.add)
                kb.ts("dve", f["m"][:], f["m"][:], 1e-30, ALU.max, r=["L_m"], w=["L_m"])
                kb.act(f["m"][:], f["m"][:], AF.Sqrt, r=["L_m"], w=["L_m"])
                kb.tt("pool", f["b"][:], f["i"][:], xc[:, t0:t0 + BLK], ALU.mult, r=["L_i", "L_xc"], w=["L_b"])
                kb.tt("pool", f["b"][:], f["b"][:], f["m"][:], ALU.mult, r=["L_b", "L_m"], w=["L_b"])
                init = 0.0 if blk == 0 else hb[:, t0 - 1:t0]
                kb.S.op("dve", lambda e, t0=t0, init=init: e.tensor_tensor_scan(out=hb[:, t0:t0 + BLK], data0=f["a"][:], data1=f["b"][:],
                                                                             initial=init, op0=ALU.mult, op1=ALU.add),
                        r=["L_a", "L_b", "L_h"], w=["L_h"])
            kb.st(d["hlruTc"][bi * 128:(bi + 1) * 128, :], hb[:], r=["L_h"])


def phase_B_attn(kb, cfg, d):
    S = kb.S
    SEQ, HPC = cfg["S"], cfg["H"] // cfg["NC"]
    QKC, KVC = cfg["QL"] // 128, cfg["KVL"] // 128
    BLK = 512
    NT = SEQ // 128
    sc = 192.0 ** -0.5
    with S.scope():
        qn = S.sbuf("T_qn", [128, SEQ], BF16)
        QR = S.sbuf("T_QR", [64, SEQ], BF16)
        kn = S.sbuf("T_kn", [128, SEQ], BF16)
        KR = S.sbuf("T_KR", [64, SEQ], BF16)
        v = S.sbuf("T_v", [128, NT, 129], BF16)
        Ssb = S.sbuf("T_S", [128, SEQ], F32)
        Psb = S.sbuf("T_P", [128, SEQ], BF16)
        PT = S.sbuf("T_PT", [128, SEQ], BF16)
        cqb = S.sbuf("T_cqb", [128, QKC, BLK], BF16)
        ckb = S.sbuf("T_ckb", [128, KVC, BLK], BF16)
        wst = S.sbuf("T_wst", [128, QKC, 128], F32)
        wq = S.sbuf("T_wq", [128, QKC, 128], BF16)
        wr1 = S.sbuf("T_wr1", [128, QKC, 64], BF16)
        wr2 = S.sbuf("T_wr2", [128, QKC, 64], BF16)
        wk = S.sbuf("T_wk", [128, KVC, 128], BF16)
        wv = S.sbuf("T_wv", [128, KVC, 128], BF16)
        ident = S.sbuf("T_ident", [128, 128], BF16)
        kb.ld(ident[:], d["c_ident"], w=["T_ident"])
        mask = S.sbuf("T_mask", [128, 128], F32)
        kb.ld(mask[:], d["c_mask"], w=["T_mask"])
        invf64 = S.sbuf("T_invf64", [64, 1], F32)
        kb.ld(invf64[:], d["c_invf64"], w=["invf"])
        sgn = S.sbuf("T_sgn", [64, 1], F32)
        kb.ld(sgn[:], d["c_sgn"], w=["T_sgn"])
        cos = S.sbuf("T_cos", [64, BLK], F32)
        sin = S.sbuf("T_sin", [64, BLK], F32)
        tmpi = S.sbuf("T_tmpi", [64, BLK], I32)
        tmpf = [S.sbuf("T_tmpf%d" % i, [64, BLK], F32) for i in range(3)]
        t1 = S.sbuf("T_t1", [64, BLK], F32)
        t2 = S.sbuf("T_t2", [64, BLK], F32)
        sm = S.sbuf("T_sm", [128, 4], F32)
        osb = [S.sbuf("T_o%d" % i, [128, 128], F32) for i in range(2)]
        pf = [S.psum("T_pf%d" % i, [128, 512], F32) for i in range(5)]
        pb = [S.psum("T_pb%d" % i, [128, 1024], BF16) for i in range(2)]
        kb.memset("pool", v[:, :, 128:129], 1.0, w=["T_vone"])
        kb.ld(KR[:], d["krT"], w=["T_KR"])

        def loadw(dst, dkey, src, KCn, M):
            kb.ld(wst[:, :KCn, :M], src, w=["T_wst"])
            kb.copy(kb.rr_eng(), dst[:, :KCn, :M], wst[:, :KCn, :M], r=["T_wst"], w=[dkey])

        for h in range(HPC):
            loadw(wq, "T_wq", d["wq_n"][h], QKC, 128)
            loadw(wr1, "T_wr1", d["wq_r1"][h], QKC, 64)
            loadw(wr2, "T_wr2", d["wq_r2"][h], QKC, 64)
            loadw(wk, "T_wk", d["wk"][h], KVC, 128)
            loadw(wv, "T_wv", d["wv"][h], KVC, 128)
            for blk in range(SEQ // BLK if not (h > 0 and cfg.get("dbg_noproj")) else 0):
                t0 = blk * BLK
                kb.ld(cqb[:], d["cqnT"][blk], w=["T_cqb"])
                kb.ld(ckb[:], d["ckvnT"][blk], w=["T_ckb"])
                rope_tables(kb, cfg, d["pos"], t0, BLK, cos, sin, invf64, tmpi, tmpf, "T_rt", P=64)
                kb.ts("dve", cos[:, :], cos[:, :], sc, ALU.mult, r=["T_rt_c"], w=["T_rt_c"])
                kb.ts("dve", sin[:, :], sin[:, :], sgn[:, 0:1], ALU.mult, r=["T_rt_s", "T_sgn"], w=["T_rt_s"], s2=sc, op1=ALU.mult)
                kb.mm_group(pf[0][:, :BLK], [(wq[:, kc, :], cqb[:, kc, :]) for kc in range(QKC)], r=["T_wq", "T_cqb"], w=["T_pf0"])
                kb.act(qn[:, t0:t0 + BLK], pf[0][:, :BLK], AF.Copy, r=["T_pf0"], w=["T_qn"], scale=sc)
                kb.mm_group(pf[1][:64, :BLK], [(wr1[:, kc, :], cqb[:, kc, :]) for kc in range(QKC)], r=["T_wr1", "T_cqb"], w=["T_pf1"])
                kb.mm_group(pf[2][:64, :BLK], [(wr2[:, kc, :], cqb[:, kc, :]) for kc in range(QKC)], r=["T_wr2", "T_cqb"], w=["T_pf2"])
                kb.tt("dve", t1[:, :], pf[1][:64, :BLK], cos[:, :], ALU.mult, r=["T_pf1", "T_rt_c"], w=["T_t1"])
                kb.tt("dve", t2[:, :], pf[2][:64, :BLK], sin[:, :], ALU.mult, r=["T_pf2", "T_rt_s"], w=["T_t2"])
                kb.tt("pool", QR[:, t0:t0 + BLK], t1[:, :], t2[:, :], ALU.add, r=["T_t1", "T_t2"], w=["T_QR"])
                kb.mm_group(pf[3][:, :BLK], [(wk[:, kc, :], ckb[:, kc, :]) for kc in range(KVC)], r=["T_wk", "T_ckb"], w=["T_pf3"])
                kb.copy("act", kn[:, t0:t0 + BLK], pf[3][:, :BLK], r=["T_pf3"], w=["T_kn"])
                for j in range(BLK // 128):
                    kb.mm_group(pf[4][:, :128], [(ckb[:, kc, j * 128:(j + 1) * 128], wv[:, kc, :]) for kc in range(KVC)],
                                r=["T_wv", "T_ckb"], w=["T_pf4"])
                    kb.copy("dve", v[:, blk * (BLK // 128) + j, 0:128], pf[4][:, :128], r=["T_pf4"], w=["T_v"])
            for i in range(NT if not (h > 0 and cfg.get("dbg_noattn")) else 0):
                nk = (i + 1) * 128
                q0 = i * 128
                nblk = (nk + BLK - 1) // BLK
                for b_ in range(nblk):
                    k0 = b_ * BLK
                    n = min(BLK, nk - k0)
                    sp, spk = pf[b_ % 2], "T_pf%d" % (b_ % 2)
                    kb.mm_group(sp[:, :n], [(qn[:, q0:q0 + 128], kn[:, k0:k0 + n]), (QR[:, q0:q0 + 128], KR[:, k0:k0 + n])],
                                r=["T_qn", "T_kn", "T_QR", "T_KR"], w=[spk])
                    if b_ == nblk - 1:
                        if n > 128:
                            kb.copy(kb.rr_eng(), Ssb[:, k0:k0 + n - 128], sp[:, :n - 128], r=[spk], w=["T_S"])
                        kb.tt("dve", Ssb[:, nk - 128:nk], sp[:, n - 128:n], mask[:], ALU.add, r=[spk, "T_mask"], w=["T_S"])
                    else:
                        kb.copy(kb.rr_eng(), Ssb[:, k0:k0 + n], sp[:, :n], r=[spk], w=["T_S"])
                skeys = ["T_S"]
                kb.S.op("dve", lambda e, nk=nk: e.reduce_max(out=sm[:, 0:1], in_=Ssb[:, :nk], axis=AX.X), r=skeys, w=["T_mx"])
                kb.ts("dve", sm[:, 1:2], sm[:, 0:1], -1.0, ALU.mult, r=["T_mx"], w=["T_nmx"])
                kb.act(Psb[:, :nk], Ssb[:, :nk], AF.Exp, r=skeys + ["T_nmx"], w=["T_P"], bias=sm[:, 1:2])
                for g in range((i + 4) // 4):
                    nt_ = min(4, i + 1 - 4 * g)
                    pt, ptk = pb[g % 2], "T_pb%d" % (g % 2)
                    pe_seq(kb, [(lambda e, g=g, j=j, pt=pt: e.transpose(out=pt[:, j * 128:(j + 1) * 128],
                                                                      in_=Psb[:, (4 * g + j) * 128:(4 * g + j + 1) * 128], identity=ident[:]))
                                for j in range(nt_)], r=["T_P", "T_ident"], w=[ptk])
                    kb.copy(kb.rr_eng(("act", "dve", "pool")[:2]), PT[:, 4 * g * 128:(4 * g + nt_) * 128], pt[:, :nt_ * 128], r=[ptk], w=["T_PT%d" % g])
                ptkeys = ["T_PT%d" % g for g in range((i + 4) // 4)]
                kb.mm_group(pf[2][:, :129], [(PT[:, kt * 128:(kt + 1) * 128], v[:, kt, :]) for kt in range(i + 1)],
                            r=ptkeys + ["T_v", "T_vone"], w=["T_pf2"])
                kb.S.op("dve", lambda e: e.reciprocal(out=sm[:, 2:3], in_=pf[2][:, 128:129]), r=["T_pf2"], w=["T_rs"])
                o, ok = osb[i % 2], "T_o%d" % (i % 2)
                kb.ts("dve", o[:], pf[2][:, :128], sm[:, 2:3], ALU.mult, r=["T_pf2", "T_rs"], w=[ok])
                kb.st(d["oc"][h, q0:q0 + 128, :], o[:], r=[ok])


def build_B(cfg):
    nc = bass.Bass("TRN2", target_bir_lowering=False)
    kb = KB(nc)
    SEQ, NC = cfg["S"], cfg["NC"]
    CPC, LPC, HPC = cfg["DC"] // NC, cfg["DL"] // NC, cfg["H"] // NC
    QL, KVL = cfg["QL"], cfg["KVL"]
    QKC, KVC = QL // 128, KVL // 128
    nb = LPC // 128
    d = {
        "gTc": kb.dram_in("gTc", [CPC, SEQ]), "convw": kb.dram_in("convw", [128, CPC // 128, 31]), "convb": kb.dram_in("convb", [128, CPC // 128]),
        "yconvTc": kb.dram_out("yconvTc", [CPC, SEQ]),
        "uTc": kb.dram_in("uTc", [LPC, SEQ]), "lruw": kb.dram_in("lruw", [128, nb, 4]), "lrub": kb.dram_in("lrub", [128, nb]),
        "ba": kb.dram_in("ba", [128, nb]), "bx": kb.dram_in("bx", [128, nb]), "lam": kb.dram_in("lam", [128, nb]),
        "lwa": kb.dram_in("lwa", [nb, 128, 128]), "lwx": kb.dram_in("lwx", [nb, 128, 128]),
        "hlruTc": kb.dram_out("hlruTc", [LPC, SEQ]),
        "cqnT": kb.dram_in("cqnT", [SEQ // 512, 128, QKC, 512], BF16), "ckvnT": kb.dram_in("ckvnT", [SEQ // 512, 128, KVC, 512], BF16), "krT": kb.dram_in("krT", [64, SEQ], BF16),
        "pos": kb.dram_in("pos", [1, SEQ], I32),
        "wq_n": kb.dram_in("wq_n", [HPC, 128, QKC, 128]), "wq_r1": kb.dram_in("wq_r1", [HPC, 128, QKC, 64]),
        "wq_r2": kb.dram_in("wq_r2", [HPC, 128, QKC, 64]), "wk": kb.dram_in("wk", [HPC, 128, KVC, 128]), "wv": kb.dram_in("wv", [HPC, 128, KVC, 128]),
        "c_ident": kb.dram_in("c_ident", [128, 128], BF16), "c_mask": kb.dram_in("c_mask", [128, 128]),
        "c_invf64": kb.dram_in("c_invf64", [64, 1]), "c_sgn": kb.dram_in("c_sgn", [64, 1]),
        "oc": kb.dram_out("oc", [HPC, SEQ, 128]),
    }
    parts = cfg.get("bparts", "cla")
    if "c" in parts:
        phase_B_conv(kb, cfg, d)
    if "l" in parts:
        phase_B_lru(kb, cfg, d)
    if "a" in parts:
        phase_B_attn(kb, cfg, d)
    kb.S.emit()
    return nc


def blockify(aT, blk=512):
    F, S = aT.shape
    return np.ascontiguousarray(aT.reshape(F // 128, 128, S // blk, blk).transpose(2, 1, 0, 3))


def prep_B(cfg, p, c):
    NC = cfg["NC"]
    CPC, LPC, HPC = cfg["DC"] // NC, cfg["DL"] // NC, cfg["H"] // NC
    nb = LPC // 128
    cs_, ls_ = slice(c * CPC, (c + 1) * CPC), slice(c * LPC, (c + 1) * LPC)
    invf = consts()["c_invf"]
    d = {"convw": np.ascontiguousarray(p["conv_dw_w"][:, cs_].T.reshape(CPC // 128, 128, 31).transpose(1, 0, 2)),
         "convb": pvec(p["conv_dw_b"][cs_]),
         "lruw": np.ascontiguousarray(p["lru_conv_w"][:, ls_].T.reshape(nb, 128, 4).transpose(1, 0, 2)),
         "lrub": pvec(p["lru_conv_b"][ls_]), "ba": pvec(p["lru_b_a"][ls_]), "bx": pvec(p["lru_b_x"][ls_]), "lam": pvec(p["lru_lambda"][ls_]),
         "lwa": np.ascontiguousarray(p["lru_w_a"][c * nb:(c + 1) * nb]), "lwx": np.ascontiguousarray(p["lru_w_x"][c * nb:(c + 1) * nb]),
         "c_ident": np.eye(128, dtype=np.float32).astype(NPBF),
         "c_mask": np.where(np.arange(128)[None, :] <= np.arange(128)[:, None], 0.0, NEG).astype(np.float32),
         "c_invf64": np.concatenate([invf, invf], 0), "c_sgn": np.concatenate([-np.ones((32, 1)), np.ones((32, 1))], 0).astype(np.float32)}
    wqn, wr1, wr2, wk, wv = [], [], [], [], []
    for hh in range(c * HPC, (c + 1) * HPC):
        q0 = hh * 192
        wqn.append(chunkify(p["w_uq"][:, q0:q0 + 128], 128)[0])
        r = p["w_uq"][:, q0 + 128:q0 + 192]
        wr1.append(chunkify(r, 64)[0])
        wr2.append(chunkify(np.concatenate([r[:, 32:], r[:, :32]], 1), 64)[0])
        k0 = hh * 256
        wk.append(chunkify(p["w_ukv"][:, k0:k0 + 128], 128)[0])
        wv.append(chunkify(p["w_ukv"][:, k0 + 128:k0 + 256], 128)[0])
    d.update({"wq_n": np.stack(wqn), "wq_r1": np.stack(wr1), "wq_r2": np.stack(wr2), "wk": np.stack(wk), "wv": np.stack(wv)})
    return d


LAYER_KEYS = ("pre_norm_g", "w_in", "conv_dw_w", "conv_dw_b", "conv_ln_g", "conv_ln_b", "w_conv_proj", "q_norm_g", "w_uq",
              "kv_norm_g", "w_ukv", "w_mla_proj", "lru_conv_w", "lru_conv_b", "lru_w_a", "lru_b_a", "lru_w_x", "lru_b_x",
              "lru_lambda", "w_lru_proj", "w_out", "post_norm_g")

FULL_CFG = dict(NC=8, S=8192, T=1024, TP=512, TPC=256, D=4096, DC=2048, QL=1536, KVL=512, H=32, DL=2048)

_PROGS = {}


def _prog(name, cfg):
    key = (name, tuple(sorted(cfg.items())))
    if key not in _PROGS:
        _PROGS[key] = {"A": build_A, "B": build_B, "C": build_C}[name](cfg)
    return _PROGS[key]


def run_model(cfg, inp, depth, runner=None):
    runner = runner or (lambda nc, maps: run(nc, maps, cfg["NC"]))
    NC, S, T = cfg["NC"], cfg["S"], cfg["T"]
    CPC, LPC = cfg["DC"] // NC, cfg["DL"] // NC
    ca = np.ascontiguousarray
    x = np.asarray(inp["x"]).reshape(S, cfg["D"])
    pos = ca(np.asarray(inp["positions"]).reshape(1, S).astype(np.int32))
    xT = ca(x.T)
    for l in range(depth):
        p = {k: np.asarray(inp[k][l]) for k in LAYER_KEYS}
        xs = [ca(xT[:, c * T:(c + 1) * T]) for c in range(NC)]
        base = prep_A(cfg, p["w_in"], p["pre_norm_g"], p["q_norm_g"], p["kv_norm_g"])
        res = runner(_prog("A", cfg), [dict(base, xT=xs[c], pos=ca(pos[:, c * T:(c + 1) * T])) for c in range(NC)])
        del base
        cat1 = lambda n: ca(np.concatenate([np.asarray(r[n]) for r in res], 1))
        gT, uT, cqnT, ckvnT = cat1("gT"), cat1("uT"), cat1("cqnT"), cat1("ckvnT")
        krT = ca(np.concatenate([cat1("kr1T"), cat1("kr2T")], 0))
        cqnT, ckvnT = blockify(cqnT), blockify(ckvnT)
        maps = [dict(prep_B(cfg, p, c), gTc=ca(gT[c * CPC:(c + 1) * CPC]), uTc=ca(uT[c * LPC:(c + 1) * LPC]),
                     cqnT=cqnT, ckvnT=ckvnT, krT=krT, pos=pos) for c in range(NC)]
        res = runner(_prog("B", cfg), maps)
        del maps, gT, uT
        yconvT = ca(np.concatenate([np.asarray(r["yconvTc"]) for r in res], 0))
        hlruT = ca(np.concatenate([np.asarray(r["hlruTc"]) for r in res], 0))
        oT = ca(np.concatenate([np.asarray(r["oc"]) for r in res], 0).transpose(0, 2, 1).reshape(-1, S))
        base = prep_C(cfg, p)
        sl = lambda a, c: ca(a[:, c * T:(c + 1) * T])
        res = runner(_prog("C", cfg), [dict(base, xT=xs[c], yconvT=sl(yconvT, c), oT=sl(oT, c), hlruT=sl(hlruT, c)) for c in range(NC)])
        del base
        xT = ca(np.concatenate([np.asarray(r["xTn"]) for r in res], 1))
    return ca(xT.T).astype(np.float32)


def kernel(**inputs):
    out = run_model(FULL_CFG, inputs, 4)
    return out.reshape(1, FULL_CFG["S"], FULL_CFG["D"])
```

```python
import contextlib
import math
import numpy as np
import ml_dtypes
import concourse.bass as bass
import concourse.mybir as mybir
from concourse.bass_utils import run_bass_kernel_spmd

F32 = mybir.dt.float32
BF16 = mybir.dt.bfloat16
I32 = mybir.dt.int32
AF = mybir.ActivationFunctionType
ALU = mybir.AluOpType
AX = mybir.AxisListType
NPBF = ml_dtypes.bfloat16

COMPUTE = ("pe", "act", "dve", "pool")
QUEUES = ("sp", "act", "pool")
NDMASEM = 8
EPS = 1e-6
NEG = -30000.0


class Sched:
    def __init__(self, nc):
        self.nc = nc
        self.stack = contextlib.ExitStack()
        self.streams = {e: [] for e in ("pe", "act", "dve", "pool", "sp")}
        self.cnt = {e: 0 for e in COMPUTE}
        self.sem = {e: self.stack.enter_context(nc.semaphore("s_" + e)) for e in COMPUTE}
        self.dsem = {q: [self.stack.enter_context(nc.semaphore("d_%s%d" % (q, i))) for i in range(NDMASEM)]
                     for q in QUEUES}
        self.dcnt = {q: 0 for q in QUEUES}
        self.dtok = {q: [None] * NDMASEM for q in QUEUES}
        self.res = {}
        self.waited = {e: {} for e in self.streams}
        self.scopes = []

    def sbuf(self, name, shape, dt):
        st = self.scopes[-1] if self.scopes else self.stack
        return st.enter_context(self.nc.sbuf_tensor(name, list(shape), dt))

    def psum(self, name, shape, dt):
        st = self.scopes[-1] if self.scopes else self.stack
        return st.enter_context(self.nc.psum_tensor(name, list(shape), dt))

    @contextlib.contextmanager
    def scope(self):
        st = contextlib.ExitStack()
        self.scopes.append(st)
        try:
            yield
        finally:
            self.barrier()
            self.scopes.pop()
            st.close()

    def _need(self, eng, toks, nosame=False):
        best = {}
        for (s, v, e) in toks:
            if nosame and e == eng:
                continue
            if s.name not in best or best[s.name][1] < v:
                best[s.name] = (s, v)
        waits = []
        for key, (s, v) in best.items():
            if self.waited[eng].get(key, 0) >= v:
                continue
            self.waited[eng][key] = v
            waits.append((s, v))
        return waits

    def _deps(self, eng, r, w, nosame=False):
        toks = []
        for k in r:
            st = self.res.get(k)
            if st and st[0] is not None:
                toks.append(st[0])
        for k in w:
            st = self.res.get(k)
            if st:
                if st[0] is not None:
                    toks.append(st[0])
                toks.extend(st[1])
        return self._need(eng, toks, nosame)

    def _commit(self, tok, r, w):
        for k in r:
            self.res.setdefault(k, [None, []])[1].append(tok)
        for k in w:
            self.res[k] = [tok, []]

    def op(self, eng, fn, r=(), w=(), nowait=False, nosame=False):
        waits = [] if nowait else self._deps(eng, r, w, nosame or eng == "pe")
        self.cnt[eng] += 1
        tok = (self.sem[eng], self.cnt[eng], eng)
        self.streams[eng].append((waits, fn, (self.sem[eng], 1)))
        self._commit(tok, r, w)
        return tok

    def dma(self, q, fn, r=(), w=()):
        i = self.dcnt[q]
        self.dcnt[q] += 1
        slot = i % NDMASEM
        s = self.dsem[q][slot]
        val = 16 * (i // NDMASEM + 1)
        waits = self._deps(q, r, w)
        prev = self.dtok[q][slot]
        if prev is not None and self.waited[q].get(s.name, 0) < prev[1]:
            self.waited[q][s.name] = prev[1]
            waits.append((s, prev[1]))
        tok = (s, val, "dma_" + q)
        self.dtok[q][slot] = tok
        self.streams[q].append((waits, fn, (s, 16)))
        self._commit(tok, r, w)
        return tok

    def _all_tokens(self):
        toks = []
        for q in QUEUES:
            for t in self.dtok[q]:
                if t is not None:
                    toks.append(t)
        for e in COMPUTE:
            if self.cnt[e]:
                toks.append((self.sem[e], self.cnt[e], e))
        return toks

    def barrier(self):
        toks = self._all_tokens()
        for e in self.streams:
            waits = self._need(e, toks)
            if waits:
                self.streams[e].append((waits, None, None))

    def emit(self):
        nc = self.nc
        self.barrier()
        streams = self.streams

        def run(name):
            def f(eng):
                for (waits, fn, inc) in streams[name]:
                    for (s, v) in waits:
                        eng.wait_ge(s, v)
                    if fn is not None:
                        fn(eng).then_inc(inc[0], inc[1])
            return f

        with nc.Block() as block:
            block.sync(run("sp"))
            block.tensor(run("pe"))
            block.scalar(run("act"))
            block.vector(run("dve"))
            block.gpsimd(run("pool"))
        self.stack.close()


class KB:
    def __init__(self, nc):
        self.nc = nc
        self.S = Sched(nc)
        self.uid = 0
        self.rr = 0

    def name(self, p):
        self.uid += 1
        return "%s_%d" % (p, self.uid)

    def dram_in(self, name, shape, dt=F32):
        return self.nc.dram_tensor(name, list(shape), dt, kind="ExternalInput").ap()

    def dram_out(self, name, shape, dt=F32):
        return self.nc.dram_tensor(name, list(shape), dt, kind="ExternalOutput").ap()

    def ld(self, out, in_, w, r=(), q="sp"):
        return self.S.dma(q, lambda e: e.dma_start(out=out, in_=in_), r=r, w=w)

    def st(self, out, in_, r, w=(), q="sp"):
        return self.S.dma(q, lambda e: e.dma_start(out=out, in_=in_), r=r, w=w)

    def act(self, out, in_, func, r, w, bias=0.0, scale=1.0):
        return self.S.op("act", lambda e: e.activation(out=out, in_=in_, func=func, bias=bias, scale=scale), r=r, w=w)

    def copy(self, eng, out, in_, r, w):
        if eng == "act":
            return self.act(out, in_, AF.Copy, r, w)
        return self.S.op(eng, lambda e: e.tensor_copy(out=out, in_=in_), r=r, w=w)

    def tt(self, eng, out, a, b, op, r, w):
        return self.S.op(eng, lambda e: e.tensor_tensor(out=out, in0=a, in1=b, op=op), r=r, w=w)

    def ts(self, eng, out, a, s1, op0, r, w, s2=None, op1=None):
        if op1 is None:
            return self.S.op(eng, lambda e: e.tensor_scalar(out=out, in0=a, scalar1=s1, scalar2=None, op0=op0), r=r, w=w)
        return self.S.op(eng, lambda e: e.tensor_scalar(out=out, in0=a, scalar1=s1, scalar2=s2, op0=op0, op1=op1), r=r, w=w)

    def stt(self, out, in0, scalar, in1, op0, op1, r, w):
        return self.S.op("dve", lambda e: e.scalar_tensor_tensor(out=out, in0=in0, scalar=scalar, in1=in1, op0=op0, op1=op1), r=r, w=w)

    def memset(self, eng, ap, val, w):
        return self.S.op(eng, lambda e: e.memset(ap, val), w=w)

    def mm_group(self, out, pairs, r, w):
        n = len(pairs)
        for i, (l, rh) in enumerate(pairs):
            fn = (lambda e, l=l, rh=rh, i=i: e.matmul(out, lhsT=l, rhs=rh, start=(i == 0), stop=(i == n - 1)))
            if i == 0:
                self.S.op("pe", fn, r=r, w=w, nosame=False)
            elif i == n - 1:
                self.S.op("pe", fn, r=r, w=w, nowait=True)
            else:
                self.S.op("pe", fn, nowait=True)

    def rr_eng(self, choices=("act", "dve")):
        self.rr += 1
        return choices[self.rr % len(choices)]


def chunkify(W, M):
    K, n = W.shape
    return np.ascontiguousarray(W.reshape(K // 128, 128, n // M, M).transpose(2, 1, 0, 3))


def pvec(v):
    return np.ascontiguousarray(v.reshape(-1, 128).T)


class WStream:
    def __init__(self, kb, KCmax, Mmax, nbuf=2, tag="w"):
        self.kb = kb
        self.n = nbuf
        self.i = 0
        self.tag = kb.name(tag)
        self.st = [kb.S.sbuf("%s_st%d" % (self.tag, i), [128, KCmax, Mmax], F32) for i in range(nbuf)]
        self.bf = [kb.S.sbuf("%s_bf%d" % (self.tag, i), [128, KCmax, Mmax], BF16) for i in range(nbuf)]

    def get(self, src, KC, M):
        kb = self.kb
        s = self.i % self.n
        self.i += 1
        ks, kbk = "%s_st%d" % (self.tag, s), "%s_bf%d" % (self.tag, s)
        kb.ld(self.st[s][:, :KC, :M], src, w=[ks])
        kb.copy(kb.rr_eng(), self.bf[s][:, :KC, :M], self.st[s][:, :KC, :M], r=[ks], w=[kbk])
        return self.bf[s], kbk


def make_hT(kb, cfg, xT, t0, TP, hT, hkey, g_sb, ones, ps_ss, x3, tmp):
    D = cfg["D"]
    KC = D // 128
    sq = tmp["sq"]
    for kc in range(KC):
        b = kc % len(x3)
        kb.ld(x3[b][:, :TP], xT[kc * 128:(kc + 1) * 128, t0:t0 + TP], w=["x3_%d" % b])
        kb.act(sq[kc % 2][:, :TP], x3[b][:, :TP], AF.Square, r=["x3_%d" % b], w=["sq%d" % (kc % 2)])
        kb.S.op("pe", lambda e, kc=kc: e.matmul(ps_ss[0][:, :TP], lhsT=ones[:], rhs=sq[kc % 2][:, :TP],
                                                 start=(kc == 0), stop=(kc == KC - 1)),
                r=["sq%d" % (kc % 2), "ones"], w=[ps_ss[1]])
    rstd = tmp["rstd"]
    kb.act(rstd[:, :TP], ps_ss[0][:, :TP], AF.Sqrt, r=[ps_ss[1], "epsc"], w=["rstd"], bias=tmp["eps"][:, 0:1], scale=1.0 / D)
    kb.S.op("dve", lambda e: e.reciprocal(out=rstd[:, :TP], in_=rstd[:, :TP]), r=["rstd"], w=["rstd"])
    for kc in range(KC):
        b = kc % len(x3)
        kb.ld(x3[b][:, :TP], xT[kc * 128:(kc + 1) * 128, t0:t0 + TP], w=["x3_%d" % b])
        kb.stt(hT[:, kc, :TP], x3[b][:, :TP], g_sb[:, kc:kc + 1], rstd[:, :TP], ALU.mult, ALU.mult,
               r=["x3_%d" % b, "rstd", "gpre"], w=[hkey])


def rope_tables(kb, cfg, pos_dram, t0, n, cos, sin, invf, tmpi, tmpf, key, P=32):
    PI, TWO_PI = math.pi, 2.0 * math.pi
    C1 = 6.28125
    C2 = TWO_PI - C1
    A, Kf, M = tmpf[0], tmpf[1], tmpf[2]
    kA, kK, kM = key + "_A", key + "_K", key + "_M"
    kb.ld(tmpi[:P, :n], pos_dram[0:1, t0:t0 + n].partition_broadcast(P), w=[key + "_pi"])
    kb.copy("dve", A[:P, :n], tmpi[:P, :n], r=[key + "_pi"], w=[kA])
    kb.ts("dve", A[:P, :n], A[:P, :n], invf[:P, 0:1], ALU.mult, r=[kA, "invf"], w=[kA])
    kb.ts("dve", Kf[:P, :n], A[:P, :n], 1.0 / TWO_PI, ALU.mult, r=[kA], w=[kK])
    kb.copy("dve", tmpi[:P, :n], Kf[:P, :n], r=[kK], w=[key + "_pi"])
    kb.copy("dve", Kf[:P, :n], tmpi[:P, :n], r=[key + "_pi"], w=[kK])
    kb.stt(A[:P, :n], Kf[:P, :n], -C1, A[:P, :n], ALU.mult, ALU.add, r=[kK, kA], w=[kA])
    kb.stt(A[:P, :n], Kf[:P, :n], -C2, A[:P, :n], ALU.mult, ALU.add, r=[kK, kA], w=[kA])

    def wrap(T_, kT):
        kb.ts("dve", M[:P, :n], T_[:P, :n], PI, ALU.is_gt, r=[kT], w=[kM])
        kb.stt(T_[:P, :n], M[:P, :n], -TWO_PI, T_[:P, :n], ALU.mult, ALU.add, r=[kM, kT], w=[kT])
        kb.ts("dve", M[:P, :n], T_[:P, :n], -PI, ALU.is_lt, r=[kT], w=[kM])
        kb.stt(T_[:P, :n], M[:P, :n], TWO_PI, T_[:P, :n], ALU.mult, ALU.add, r=[kM, kT], w=[kT])
        kb.ts("dve", T_[:P, :n], T_[:P, :n], -3.1415925, ALU.max, r=[kT], w=[kT], s2=3.1415925, op1=ALU.min)

    wrap(A, kA)
    kb.act(sin[:P, :n], A[:P, :n], AF.Sin, r=[kA], w=[key + "_s"])
    kb.ts("dve", A[:P, :n], A[:P, :n], PI / 2.0, ALU.add, r=[kA], w=[kA])
    wrap(A, kA)
    kb.act(cos[:P, :n], A[:P, :n], AF.Sin, r=[kA], w=[key + "_c"])


def load_consts(kb, ones_d, invf_d):
    S = kb.S
    kb.ones = S.sbuf("ones", [128, 128], F32)
    kb.ld(kb.ones[:], ones_d, w=["ones"])
    kb.invf = S.sbuf("invf", [32, 1], F32)
    kb.ld(kb.invf[:], invf_d, w=["invf"])
    kb.negpi = S.sbuf("negpi", [128, 1], F32)
    kb.memset("pool", kb.negpi[:], -math.pi, w=["negpi"])
    kb.epsc = S.sbuf("epsc", [128, 1], F32)
    kb.memset("pool", kb.epsc[:], EPS, w=["epsc"])


def phase_A(kb, cfg, d):
    S = kb.S
    D, T, TP = cfg["D"], cfg["T"], cfg["TP"]
    DC, QL, KVL, DL = cfg["DC"], cfg["QL"], cfg["KVL"], cfg["DL"]
    KC = D // 128
    with S.scope():
        hT = S.sbuf("A_hT", [128, KC, TP], BF16)
        x3 = [S.sbuf("A_x3_%d" % i, [128, TP], F32) for i in range(3)]
        tmp = {"sq": [S.sbuf("A_sq%d" % i, [128, TP], F32) for i in range(2)],
               "rstd": S.sbuf("A_rstd", [128, TP], F32), "eps": kb.epsc}
        gpre = S.sbuf("A_gpre", [128, KC], F32)
        kb.ld(gpre[:], d["gpre"], w=["gpre"])
        gq = S.sbuf("A_gq", [128, QL // 128], F32)
        kb.ld(gq[:], d["gq"], w=["gq"])
        gkv = S.sbuf("A_gkv", [128, KVL // 128], F32)
        kb.ld(gkv[:], d["gkv"], w=["gkv"])
        ws = WStream(kb, KC, 128, nbuf=2, tag="Aw")
        ps = [S.psum("A_ps%d" % i, [128, 512], F32) for i in range(6)]
        cq = S.sbuf("A_cq", [128, QL // 128, TP], F32)
        ckv = S.sbuf("A_ckv", [128, KVL // 128, TP], F32)
        rq = S.sbuf("A_rq", [128, TP], F32)
        o32 = [S.sbuf("A_o32_%d" % i, [128, TP], F32) for i in range(3)]
        ob = [S.sbuf("A_ob_%d" % i, [128, TP], BF16) for i in range(2)]
        sg = S.sbuf("A_sg", [128, TP], F32)
        cos = S.sbuf("A_cos", [32, TP], F32)
        sin = S.sbuf("A_sin", [32, TP], F32)
        tmpi = S.sbuf("A_tmpi", [32, TP], I32)
        tmpf = [S.sbuf("A_tmpf%d" % i, [32, TP], F32) for i in range(3)]
        oi = [0, 0]

        def proj(src, M, pi):
            wb, wk = ws.get(src, KC, M)
            kb.mm_group(ps[pi][:M, :TP], [(wb[:, kc, :M], hT[:, kc, :TP]) for kc in range(KC)],
                        r=[wk, "A_hT"], w=["A_ps%d" % pi])

        def next_o32():
            oi[0] += 1
            return o32[oi[0] % 3], "A_o32_%d" % (oi[0] % 3)

        def next_ob():
            oi[1] += 1
            return ob[oi[1] % 2], "A_ob_%d" % (oi[1] % 2)

        for t0 in range(0, T, TP):
            make_hT(kb, cfg, d["xT"], t0, TP, hT, "A_hT", gpre, kb.ones, (ps[5], "A_ps5"), x3, tmp)
            parts = cfg.get("parts", "glqr")
            for j in range(DC // 128 if "g" in parts else 0):
                proj(d["wa_a"][j], 128, 0)
                proj(d["wa_b"][j], 128, 1)
                kb.act(sg[:, :TP], ps[1][:, :TP], AF.Sigmoid, r=["A_ps1"], w=["A_sg"])
                o, ok = next_o32()
                kb.tt("dve", o[:, :TP], ps[0][:, :TP], sg[:, :TP], ALU.mult, r=["A_ps0", "A_sg"], w=[ok])
                kb.st(d["gT"][j * 128:(j + 1) * 128, t0:t0 + TP], o[:, :TP], r=[ok])
            for j in range(DL // 128 if "l" in parts else 0):
                proj(d["wa_u"][j], 128, j % 2)
                o, ok = next_o32()
                kb.copy(kb.rr_eng(), o[:, :TP], ps[j % 2][:, :TP], r=["A_ps%d" % (j % 2)], w=[ok])
                kb.st(d["uT"][j * 128:(j + 1) * 128, t0:t0 + TP], o[:, :TP], r=[ok])
            for (wname, n, buf, bkey, gv, gk, outname) in (("wa_cq", QL, cq, "A_cq", gq, "gq", "cqnT"),
                                                            ("wa_ckv", KVL, ckv, "A_ckv", gkv, "gkv", "ckvnT")):
                if "q" not in parts:
                    break
                nch = n // 128
                for j in range(nch):
                    proj(d[wname][j], 128, j % 2)
                    kb.copy(kb.rr_eng(), buf[:, j, :TP], ps[j % 2][:, :TP], r=["A_ps%d" % (j % 2)], w=[bkey + str(j)])
                for j in range(nch):
                    kb.act(tmp["sq"][j % 2][:, :TP], buf[:, j, :TP], AF.Square, r=[bkey + str(j)], w=["sq%d" % (j % 2)])
                    kb.S.op("pe", lambda e, j=j, nch=nch: e.matmul(ps[4][:, :TP], lhsT=kb.ones[:], rhs=tmp["sq"][j % 2][:, :TP],
                                                                  start=(j == 0), stop=(j == nch - 1)),
                            r=["sq%d" % (j % 2), "ones"], w=["A_ps4"])
                kb.act(rq[:, :TP], ps[4][:, :TP], AF.Sqrt, r=["A_ps4", "epsc"], w=["A_rq"], bias=kb.epsc[:, 0:1], scale=1.0 / n)
                kb.S.op("dve", lambda e: e.reciprocal(out=rq[:, :TP], in_=rq[:, :TP]), r=["A_rq"], w=["A_rq"])
                for j in range(nch):
                    o, ok = next_ob()
                    kb.stt(o[:, :TP], buf[:, j, :TP], gv[:, j:j + 1], rq[:, :TP], ALU.mult, ALU.mult,
                           r=[bkey + str(j), "A_rq", gk], w=[ok])
                    kb.st(d[outname][j * 128:(j + 1) * 128, t0:t0 + TP], o[:, :TP], r=[ok])
            if "r" not in parts:
                continue
            rope_tables(kb, cfg, d["pos"], t0, TP, cos, sin, kb.invf, tmpi, tmpf, "A_rt")
            proj(d["wa_kr"][0], 32, 2)
            proj(d["wa_kr"][1], 32, 3)
            x1, x2 = ps[2], ps[3]
            ta, tak = next_o32()
            tb, tbk = next_o32()
            kb.tt("dve", ta[:32, :TP], x1[:32, :TP], cos[:32, :TP], ALU.mult, r=["A_ps2", "A_rt_c"], w=[tak])
            kb.tt("dve", tb[:32, :TP], x2[:32, :TP], sin[:32, :TP], ALU.mult, r=["A_ps3", "A_rt_s"], w=[tbk])
            o, ok = next_ob()
            kb.tt("dve", o[:32, :TP], ta[:32, :TP], tb[:32, :TP], ALU.subtract, r=[tak, tbk], w=[ok])
            kb.st(d["kr1T"][:, t0:t0 + TP], o[:32, :TP], r=[ok])
            ta, tak = next_o32()
            tb, tbk = next_o32()
            kb.tt("dve", ta[:32, :TP], x1[:32, :TP], sin[:32, :TP], ALU.mult, r=["A_ps2", "A_rt_s"], w=[tak])
            kb.tt("dve", tb[:32, :TP], x2[:32, :TP], cos[:32, :TP], ALU.mult, r=["A_ps3", "A_rt_c"], w=[tbk])
            o, ok = next_ob()
            kb.tt("dve", o[:32, :TP], ta[:32, :TP], tb[:32, :TP], ALU.add, r=[tak, tbk], w=[ok])
            kb.st(d["kr2T"][:, t0:t0 + TP], o[:32, :TP], r=[ok])


def build_A(cfg):
    nc = bass.Bass("TRN2", target_bir_lowering=False)
    kb = KB(nc)
    D, T = cfg["D"], cfg["T"]
    DC, QL, KVL, DL = cfg["DC"], cfg["QL"], cfg["KVL"], cfg["DL"]
    KC = D // 128
    d = {
        "xT": kb.dram_in("xT", [D, T]), "pos": kb.dram_in("pos", [1, T], I32),
        "gpre": kb.dram_in("gpre", [128, KC]), "gq": kb.dram_in("gq", [128, QL // 128]),
        "gkv": kb.dram_in("gkv", [128, KVL // 128]),
        "wa_a": kb.dram_in("wa_a", [DC // 128, 128, KC, 128]), "wa_b": kb.dram_in("wa_b", [DC // 128, 128, KC, 128]),
        "wa_cq": kb.dram_in("wa_cq", [QL // 128, 128, KC, 128]), "wa_ckv": kb.dram_in("wa_ckv", [KVL // 128, 128, KC, 128]),
        "wa_kr": kb.dram_in("wa_kr", [2, 128, KC, 32]), "wa_u": kb.dram_in("wa_u", [DL // 128, 128, KC, 128]),
        "gT": kb.dram_out("gT", [DC, T]), "uT": kb.dram_out("uT", [DL, T]),
        "cqnT": kb.dram_out("cqnT", [QL, T], BF16), "ckvnT": kb.dram_out("ckvnT", [KVL, T], BF16),
        "kr1T": kb.dram_out("kr1T", [32, T], BF16), "kr2T": kb.dram_out("kr2T", [32, T], BF16),
    }
    load_consts(kb, kb.dram_in("c_ones", [128, 128]), kb.dram_in("c_invf", [32, 1]))
    phase_A(kb, cfg, d)
    kb.S.emit()
    return nc


def col_splits(cfg):
    D, DC, QL, KVL, H, DL = cfg["D"], cfg["DC"], cfg["QL"], cfg["KVL"], cfg["H"], cfg["DL"]
    sizes = (("a", DC), ("b", DC), ("gconv", DC), ("cq", QL), ("ckv", KVL), ("kr", 64), ("gmla", H * 128),
             ("u", DL), ("glru", DL), ("gm0", D), ("gm1", D), ("gm2", D))
    off, o = {}, 0
    for n, s in sizes:
        off[n] = (o, o + s)
        o += s
    return off


def consts():
    invf = (10000.0 ** (-np.arange(0, 64, 2, dtype=np.float32) / 64)).astype(np.float32).reshape(32, 1)
    return {"c_ones": np.ones((128, 128), np.float32), "c_invf": invf}


def prep_A(cfg, w_in, pre_g, q_g, kv_g):
    cs = col_splits(cfg)
    sl = lambda n: w_in[:, cs[n][0]:cs[n][1]]
    d = {"gpre": pvec(pre_g), "gq": pvec(q_g), "gkv": pvec(kv_g),
         "wa_a": chunkify(sl("a"), 128), "wa_b": chunkify(sl("b"), 128), "wa_cq": chunkify(sl("cq"), 128),
         "wa_ckv": chunkify(sl("ckv"), 128), "wa_kr": chunkify(sl("kr"), 32), "wa_u": chunkify(sl("u"), 128)}
    d.update(consts())
    return d


def run(nc, in_maps, ncores):
    res = run_bass_kernel_spmd(nc, in_maps, core_ids=list(range(ncores)))
    return res.results


def phase_C(kb, cfg, d):
    S = kb.S
    D, T, TP = cfg["D"], cfg["T"], cfg["TPC"]
    DC, DL, HV = cfg["DC"], cfg["DL"], cfg["H"] * 128
    KC = D // 128
    NA, NB, NL = DC // 128, HV // 128, DL // 128
    with S.scope():
        hT = S.sbuf("C_hT", [128, KC, TP], BF16)
        zA = S.sbuf("C_zA", [128, NA, TP], BF16)
        zB = S.sbuf("C_zB", [128, NB, TP], BF16)
        zC = S.sbuf("C_zC", [128, NL, TP], BF16)
        mT = S.sbuf("C_mT", [128, KC, TP], BF16)
        x3 = [S.sbuf("C_x3_%d" % i, [128, TP], F32) for i in range(2)]
        tmp = {"sq": [S.sbuf("C_sq%d" % i, [128, TP], F32) for i in range(2)],
               "rstd": S.sbuf("C_rstd", [128, TP], F32), "eps": kb.epsc}
        gpre = S.sbuf("C_gpre", [128, KC], F32)
        kb.ld(gpre[:], d["gpre"], w=["gpre"])
        gpost = S.sbuf("C_gpost", [128, KC], F32)
        kb.ld(gpost[:], d["gpost"], w=["gpost"])
        lng = S.sbuf("C_lng", [128, NA], F32)
        kb.ld(lng[:], d["lng"], w=["lng"])
        lnb = S.sbuf("C_lnb", [128, NA], F32)
        kb.ld(lnb[:], d["lnb"], w=["lnb"])
        ws = WStream(kb, KC, 128, nbuf=2, tag="Cw")
        ps = [S.psum("C_ps%d" % i, [128, 512], F32) for i in range(7)]
        f = {n: S.sbuf("C_" + n, [128, TP], F32) for n in
             ("acc1", "mean", "lrstd", "t1", "t2", "sg0", "sg1", "sg2")}

        def proj(src, nk, rhsbuf, rkey, pi):
            wb, wk = ws.get(src, nk, 128)
            kb.mm_group(ps[pi][:, :TP], [(wb[:, kc, :], rhsbuf[:, kc, :TP]) for kc in range(nk)],
                        r=[wk, rkey], w=["C_ps%d" % pi])

        def onesum(src, skey, pi):
            kb.S.op("pe", lambda e: e.matmul(ps[pi][:, :TP], lhsT=kb.ones[:], rhs=src[:, :TP], start=True, stop=True),
                    r=[skey, "ones"], w=["C_ps%d" % pi])

        for t0 in range(0, T, TP):
            make_hT(kb, cfg, d["xT"], t0, TP, hT, "C_hT", gpre, kb.ones, (ps[6], "C_ps6"), x3, tmp)
            for j in range(NA):
                b = j % 2
                kb.ld(x3[b][:, :TP], d["yconvT"][j * 128:(j + 1) * 128, t0:t0 + TP], w=["x3_%d" % b])
                if j == 0:
                    kb.copy("dve", f["acc1"][:, :TP], x3[b][:, :TP], r=["x3_%d" % b], w=["C_acc1"])
                    kb.tt("pool", tmp["sq"][1][:, :TP], x3[b][:, :TP], x3[b][:, :TP], ALU.mult, r=["x3_%d" % b], w=["sq1"])
                else:
                    kb.tt("dve", f["acc1"][:, :TP], f["acc1"][:, :TP], x3[b][:, :TP], ALU.add, r=["x3_%d" % b, "C_acc1"], w=["C_acc1"])
                    kb.tt("pool", f["t1"][:, :TP], x3[b][:, :TP], x3[b][:, :TP], ALU.mult, r=["x3_%d" % b], w=["C_t1"])
                    kb.tt("pool", tmp["sq"][1][:, :TP], tmp["sq"][1][:, :TP], f["t1"][:, :TP], ALU.add, r=["C_t1", "sq1"], w=["sq1"])
            onesum(f["acc1"], "C_acc1", 4)
            onesum(tmp["sq"][1], "sq1", 5)
            kb.ts("dve", f["mean"][:, :TP], ps[4][:, :TP], 1.0 / DC, ALU.mult, r=["C_ps4"], w=["C_mean"])
            kb.tt("dve", f["t1"][:, :TP], f["mean"][:, :TP], f["mean"][:, :TP], ALU.mult, r=["C_mean"], w=["C_t1"])
            kb.stt(f["t2"][:, :TP], ps[5][:, :TP], 1.0 / DC, f["t1"][:, :TP], ALU.mult, ALU.subtract, r=["C_ps5", "C_t1"], w=["C_t2"])
            kb.act(f["lrstd"][:, :TP], f["t2"][:, :TP], AF.Sqrt, r=["C_t2", "epsc"], w=["C_lrstd"], bias=kb.epsc[:, 0:1], scale=1.0)
            kb.S.op("dve", lambda e: e.reciprocal(out=f["lrstd"][:, :TP], in_=f["lrstd"][:, :TP]), r=["C_lrstd"], w=["C_lrstd"])
            for j in range(NA):
                b = j % 2
                proj(d["wc_gconv"][j], KC, hT, "C_hT", b)
                kb.act(f["sg%d" % b][:, :TP], ps[b][:, :TP], AF.Silu, r=["C_ps%d" % b], w=["C_sg%d" % b])
                kb.ld(x3[b][:, :TP], d["yconvT"][j * 128:(j + 1) * 128, t0:t0 + TP], w=["x3_%d" % b])
                kb.tt("pool", f["t1"][:, :TP], x3[b][:, :TP], f["mean"][:, :TP], ALU.subtract, r=["x3_%d" % b, "C_mean"], w=["C_t1"])
                kb.tt("pool", f["t1"][:, :TP], f["t1"][:, :TP], f["lrstd"][:, :TP], ALU.mult, r=["C_t1", "C_lrstd"], w=["C_t1"])
                kb.act(f["t2"][:, :TP], f["t1"][:, :TP], AF.Silu, r=["C_t1", "lng", "lnb"], w=["C_t2"],
                       bias=lnb[:, j:j + 1], scale=lng[:, j:j + 1])
                kb.tt("dve", zA[:, j, :TP], f["t2"][:, :TP], f["sg%d" % b][:, :TP], ALU.mult, r=["C_t2", "C_sg%d" % b], w=["C_zA"])
            for (wname, src, n, zbuf, zkey) in (("wc_gmla", "oT", NB, zB, "C_zB"), ("wc_glru", "hlruT", NL, zC, "C_zC")):
                for j in range(n):
                    b = j % 2
                    proj(d[wname][j], KC, hT, "C_hT", b)
                    kb.act(f["sg%d" % b][:, :TP], ps[b][:, :TP], AF.Silu, r=["C_ps%d" % b], w=["C_sg%d" % b])
                    kb.ld(x3[b][:, :TP], d[src][j * 128:(j + 1) * 128, t0:t0 + TP], w=["x3_%d" % b])
                    kb.tt("dve" if j % 2 else "pool", zbuf[:, j, :TP], x3[b][:, :TP], f["sg%d" % b][:, :TP], ALU.mult,
                          r=["x3_%d" % b, "C_sg%d" % b], w=[zkey])
            for m in range(KC):
                proj(d["wp_conv"][m], NA, zA, "C_zA", 0)
                proj(d["wp_mla"][m], NB, zB, "C_zB", 1)
                proj(d["wp_lru"][m], NL, zC, "C_zC", 2)
                for i in range(3):
                    proj(d["wc_gm"][i * KC + m], KC, hT, "C_hT", 3)
                    kb.act(f["sg%d" % i][:, :TP], ps[3][:, :TP], AF.Sigmoid, r=["C_ps3"], w=["C_sg%d" % i])
                kb.tt("dve", f["t1"][:, :TP], ps[0][:, :TP], f["sg0"][:, :TP], ALU.mult, r=["C_ps0", "C_sg0"], w=["C_t1"])
                kb.tt("dve", f["t2"][:, :TP], ps[1][:, :TP], f["sg1"][:, :TP], ALU.mult, r=["C_ps1", "C_sg1"], w=["C_t2"])
                kb.tt("dve", tmp["sq"][0][:, :TP], ps[2][:, :TP], f["sg2"][:, :TP], ALU.mult, r=["C_ps2", "C_sg2"], w=["sq0"])
                kb.tt("pool", f["t1"][:, :TP], f["t1"][:, :TP], f["t2"][:, :TP], ALU.add, r=["C_t1", "C_t2"], w=["C_t1"])
                kb.tt("pool", mT[:, m, :TP], f["t1"][:, :TP], tmp["sq"][0][:, :TP], ALU.add, r=["C_t1", "sq0"], w=["C_mT"])
            for m in range(KC):
                proj(d["wout"][m], KC, mT, "C_mT", m % 2)
                if m == 0:
                    kb.act(f["acc1"][:, :TP], ps[0][:, :TP], AF.Square, r=["C_ps0"], w=["C_acc1"])
                else:
                    kb.act(f["t1"][:, :TP], ps[m % 2][:, :TP], AF.Square, r=["C_ps%d" % (m % 2)], w=["C_t1"])
                    kb.tt("dve", f["acc1"][:, :TP], f["acc1"][:, :TP], f["t1"][:, :TP], ALU.add, r=["C_t1", "C_acc1"], w=["C_acc1"])
            onesum(f["acc1"], "C_acc1", 4)
            kb.act(tmp["rstd"][:, :TP], ps[4][:, :TP], AF.Sqrt, r=["C_ps4", "epsc"], w=["rstd"], bias=kb.epsc[:, 0:1], scale=1.0 / D)
            kb.S.op("dve", lambda e: e.reciprocal(out=tmp["rstd"][:, :TP], in_=tmp["rstd"][:, :TP]), r=["rstd"], w=["rstd"])
            for m in range(KC):
                b = m % 2
                proj(d["wout"][m], KC, mT, "C_mT", b)
                kb.ld(x3[b][:, :TP], d["xT"][m * 128:(m + 1) * 128, t0:t0 + TP], w=["x3_%d" % b])
                kb.stt(f["t%d" % (b + 1)][:, :TP], ps[b][:, :TP], gpost[:, m:m + 1], tmp["rstd"][:, :TP], ALU.mult, ALU.mult,
                       r=["C_ps%d" % b, "gpost", "rstd"], w=["C_t%d" % (b + 1)])
                kb.tt("pool", f["t%d" % (b + 1)][:, :TP], f["t%d" % (b + 1)][:, :TP], x3[b][:, :TP], ALU.add,
                      r=["C_t%d" % (b + 1), "x3_%d" % b], w=["C_t%d" % (b + 1)])
                kb.st(d["xTn"][m * 128:(m + 1) * 128, t0:t0 + TP], f["t%d" % (b + 1)][:, :TP], r=["C_t%d" % (b + 1)])


def build_C(cfg):
    nc = bass.Bass("TRN2", target_bir_lowering=False)
    kb = KB(nc)
    D, T = cfg["D"], cfg["T"]
    DC, DL, HV = cfg["DC"], cfg["DL"], cfg["H"] * 128
    KC = D // 128
    d = {
        "xT": kb.dram_in("xT", [D, T]), "gpre": kb.dram_in("gpre", [128, KC]), "gpost": kb.dram_in("gpost", [128, KC]),
        "lng": kb.dram_in("lng", [128, DC // 128]), "lnb": kb.dram_in("lnb", [128, DC // 128]),
        "yconvT": kb.dram_in("yconvT", [DC, T]), "oT": kb.dram_in("oT", [HV, T]), "hlruT": kb.dram_in("hlruT", [DL, T]),
        "wc_gconv": kb.dram_in("wc_gconv", [DC // 128, 128, KC, 128]), "wc_gmla": kb.dram_in("wc_gmla", [HV // 128, 128, KC, 128]),
        "wc_glru": kb.dram_in("wc_glru", [DL // 128, 128, KC, 128]), "wc_gm": kb.dram_in("wc_gm", [3 * KC, 128, KC, 128]),
        "wp_conv": kb.dram_in("wp_conv", [KC, 128, DC // 128, 128]), "wp_mla": kb.dram_in("wp_mla", [KC, 128, HV // 128, 128]),
        "wp_lru": kb.dram_in("wp_lru", [KC, 128, DL // 128, 128]), "wout": kb.dram_in("wout", [KC, 128, KC, 128]),
        "xTn": kb.dram_out("xTn", [D, T]),
    }
    load_consts(kb, kb.dram_in("c_ones", [128, 128]), kb.dram_in("c_invf", [32, 1]))
    phase_C(kb, cfg, d)
    kb.S.emit()
    return nc


def prep_C(cfg, p):
    cs = col_splits(cfg)
    w_in = p["w_in"]
    sl = lambda n: w_in[:, cs[n][0]:cs[n][1]]
    d = {"gpre": pvec(p["pre_norm_g"]), "gpost": pvec(p["post_norm_g"]), "lng": pvec(p["conv_ln_g"]), "lnb": pvec(p["conv_ln_b"]),
         "wc_gconv": chunkify(sl("gconv"), 128), "wc_gmla": chunkify(sl("gmla"), 128), "wc_glru": chunkify(sl("glru"), 128),
         "wc_gm": np.concatenate([chunkify(sl("gm0"), 128), chunkify(sl("gm1"), 128), chunkify(sl("gm2"), 128)], 0),
         "wp_conv": chunkify(p["w_conv_proj"], 128), "wp_mla": chunkify(p["w_mla_proj"], 128),
         "wp_lru": chunkify(p["w_lru_proj"], 128), "wout": chunkify(p["w_out"], 128)}
    d.update(consts())
    return d


def pe_seq(kb, fns, r, w):
    n = len(fns)
    for i, fn in enumerate(fns):
        if i == 0:
            kb.S.op("pe", fn, r=r, w=w)
        elif i == n - 1:
            kb.S.op("pe", fn, r=r, w=w, nowait=True)
        else:
            kb.S.op("pe", fn, nowait=True)


def phase_B_conv(kb, cfg, d):
    S = kb.S
    SEQ, CPC = cfg["S"], cfg["DC"] // cfg["NC"]
    PIECE = min(2048, SEQ)
    nch = CPC // 128
    with S.scope():
        gin = S.sbuf("B_gin", [128, 30 + SEQ], F32)
        acc = S.sbuf("B_acc", [128, SEQ], F32)
        cw = S.sbuf("B_cw", [128, nch, 31], F32)
        cb = S.sbuf("B_cb", [128, nch], F32)
        kb.ld(cw[:], d["convw"], w=["B_cw"])
        kb.ld(cb[:], d["convb"], w=["B_cb"])
        kb.memset("pool", gin[:, 0:30], 0.0, w=["B_gin"])
        for cc in range(nch):
            kb.ld(gin[:, 30:30 + SEQ], d["gTc"][cc * 128:(cc + 1) * 128, :], w=["B_gin"])
            for p0 in range(0, SEQ, PIECE):
                ak = "B_acc%d" % p0
                a = acc[:, p0:p0 + PIECE]
                kb.ts("dve", a, gin[:, p0:p0 + PIECE], cw[:, cc, 0:1], ALU.mult, r=["B_gin", "B_cw"], w=[ak])
                for k in range(1, 31):
                    kb.stt(a, gin[:, p0 + k:p0 + k + PIECE], cw[:, cc, k:k + 1], a, ALU.mult, ALU.add, r=["B_gin", "B_cw", ak], w=[ak])
                kb.ts("dve", a, a, cb[:, cc:cc + 1], ALU.add, r=[ak, "B_cb"], w=[ak])
                kb.st(d["yconvTc"][cc * 128:(cc + 1) * 128, p0:p0 + PIECE], a, r=[ak])


def phase_B_lru(kb, cfg, d):
    S = kb.S
    SEQ, LPC = cfg["S"], cfg["DL"] // cfg["NC"]
    PIECE = min(2048, SEQ)
    nb = LPC // 128
    BLK = 512
    with S.scope():
        uin = S.sbuf("L_uin", [128, 3 + SEQ], F32)
        xc = S.sbuf("L_xc", [128, SEQ], F32)
        xcb = S.sbuf("L_xcb", [128, SEQ], BF16)
        hb = S.sbuf("L_h", [128, SEQ], F32)
        lw = S.sbuf("L_lw", [128, nb, 4], F32)
        kb.ld(lw[:], d["lruw"], w=["L_lw"])
        vec = {}
        for n in ("lrub", "ba", "bx", "lam"):
            vec[n] = S.sbuf("L_" + n, [128, nb], F32)
            kb.ld(vec[n][:], d[n], w=["L_" + n])
        onec = S.sbuf("L_onec", [128, 1], F32)
        kb.memset("pool", onec[:], 1.0, w=["L_onec"])
        cc = S.sbuf("L_c", [128, 2], F32)
        wst = [S.sbuf("L_wst%d" % i, [128, 128], F32) for i in range(2)]
        wbf = [S.sbuf("L_wbf%d" % i, [128, 128], BF16) for i in range(2)]
        ps = [S.psum("L_ps%d" % i, [128, 512], F32) for i in range(2)]
        f = {n: S.sbuf("L_" + n, [128, BLK], F32) for n in ("r", "i", "a", "m", "b")}
        kb.memset("pool", uin[:, 0:3], 0.0, w=["L_uin"])
        for bi in range(nb):
            kb.ld(uin[:, 3:3 + SEQ], d["uTc"][bi * 128:(bi + 1) * 128, :], w=["L_uin"])
            for i, nm in enumerate(("lwa", "lwx")):
                kb.ld(wst[i][:], d[nm][bi], w=["L_wst%d" % i])
                kb.copy("act", wbf[i][:], wst[i][:], r=["L_wst%d" % i], w=["L_wbf%d" % i])
            kb.act(cc[:, 0:1], vec["lam"][:, bi:bi + 1], AF.Exp, r=["L_lam"], w=["L_c"], scale=-1.0)
            kb.act(cc[:, 0:1], cc[:, 0:1], AF.Ln, r=["L_c", "L_onec"], w=["L_c"], bias=onec[:, 0:1])
            kb.ts("dve", cc[:, 1:2], cc[:, 0:1], -16.0, ALU.mult, r=["L_c"], w=["L_c2"])
            kb.ts("dve", cc[:, 0:1], cc[:, 0:1], -8.0, ALU.mult, r=["L_c", "L_c2"], w=["L_c"])
            for p0 in range(0, SEQ, PIECE):
                a = xc[:, p0:p0 + PIECE]
                kb.ts("dve", a, uin[:, p0:p0 + PIECE], lw[:, bi, 0:1], ALU.mult, r=["L_uin", "L_lw"], w=["L_xc"])
                for k in range(1, 4):
                    kb.stt(a, uin[:, p0 + k:p0 + k + PIECE], lw[:, bi, k:k + 1], a, ALU.mult, ALU.add, r=["L_uin", "L_lw", "L_xc"], w=["L_xc"])
                kb.ts("dve", a, a, vec["lrub"][:, bi:bi + 1], ALU.add, r=["L_xc", "L_lrub"], w=["L_xc"])
                kb.copy("act", xcb[:, p0:p0 + PIECE], a, r=["L_xc"], w=["L_xcb"])
            for blk in range(SEQ // BLK):
                t0 = blk * BLK
                kb.mm_group(ps[0][:, :BLK], [(wbf[0][:], xcb[:, t0:t0 + BLK])], r=["L_wbf0", "L_xcb"], w=["L_ps0"])
                kb.mm_group(ps[1][:, :BLK], [(wbf[1][:], xcb[:, t0:t0 + BLK])], r=["L_wbf1", "L_xcb"], w=["L_ps1"])
                kb.act(f["r"][:], ps[0][:, :BLK], AF.Sigmoid, r=["L_ps0", "L_ba"], w=["L_r"], bias=vec["ba"][:, bi:bi + 1])
                kb.act(f["i"][:], ps[1][:, :BLK], AF.Sigmoid, r=["L_ps1", "L_bx"], w=["L_i"], bias=vec["bx"][:, bi:bi + 1])
                kb.act(f["a"][:], f["r"][:], AF.Exp, r=["L_r", "L_c"], w=["L_a"], scale=cc[:, 0:1])
                kb.act(f["m"][:], f["r"][:], AF.Exp, r=["L_r", "L_c2"], w=["L_m"], scale=cc[:, 1:2])
                kb.ts("dve", f["m"][:], f["m"][:], -1.0, ALU.mult, r=["L_m"], w=["L_m"], s2=1.0, op1=ALU.add)
                kb.ts("dve", f["m"][:], f["m"][:], 1e-30, ALU.max, r=["L_m"], w=["L_m"])
                kb.act(f["m"][:], f["m"][:], AF.Sqrt, r=["L_m"], w=["L_m"])
                kb.tt("pool", f["b"][:], f["i"][:], xc[:, t0:t0 + BLK], ALU.mult, r=["L_i", "L_xc"], w=["L_b"])
                kb.tt("pool", f["b"][:], f["b"][:], f["m"][:], ALU.mult, r=["L_b", "L_m"], w=["L_b"])
                init = 0.0 if blk == 0 else hb[:, t0 - 1:t0]
                kb.S.op("dve", lambda e, t0=t0, init=init: e.tensor_tensor_scan(out=hb[:, t0:t0 + BLK], data0=f["a"][:], data1=f["b"][:],
                                                                             initial=init, op0=ALU.mult, op1=ALU.add),
                        r=["L_a", "L_b", "L_h"], w=["L_h"])
            kb.st(d["hlruTc"][bi * 128:(bi + 1) * 128, :], hb[:], r=["L_h"])


def phase_B_attn(kb, cfg, d):
    S = kb.S
    SEQ, HPC = cfg["S"], cfg["H"] // cfg["NC"]
    QKC, KVC = cfg["QL"] // 128, cfg["KVL"] // 128
    BLK = 512
    NT = SEQ // 128
    sc = 192.0 ** -0.5
    with S.scope():
        qn = S.sbuf("T_qn", [128, SEQ], BF16)
        QR = S.sbuf("T_QR", [64, SEQ], BF16)
        kn = S.sbuf("T_kn", [128, SEQ], BF16)
        KR = S.sbuf("T_KR", [64, SEQ], BF16)
        v = S.sbuf("T_v", [128, NT, 129], BF16)
        Ssb = S.sbuf("T_S", [128, SEQ], F32)
        Psb = S.sbuf("T_P", [128, SEQ], BF16)
        PT = S.sbuf("T_PT", [128, SEQ], BF16)
        cqb = S.sbuf("T_cqb", [128, QKC, BLK], BF16)
        ckb = S.sbuf("T_ckb", [128, KVC, BLK], BF16)
        wst = S.sbuf("T_wst", [128, QKC, 128], F32)
        wq = S.sbuf("T_wq", [128, QKC, 128], BF16)
        wr1 = S.sbuf("T_wr1", [128, QKC, 64], BF16)
        wr2 = S.sbuf("T_wr2", [128, QKC, 64], BF16)
        wk = S.sbuf("T_wk", [128, KVC, 128], BF16)
        wv = S.sbuf("T_wv", [128, KVC, 128], BF16)
        ident = S.sbuf("T_ident", [128, 128], BF16)
        kb.ld(ident[:], d["c_ident"], w=["T_ident"])
        mask = S.sbuf("T_mask", [128, 128], F32)
        kb.ld(mask[:], d["c_mask"], w=["T_mask"])
        invf64 = S.sbuf("T_invf64", [64, 1], F32)
        kb.ld(invf64[:], d["c_invf64"], w=["invf"])
        sgn = S.sbuf("T_sgn", [64, 1], F32)
        kb.ld(sgn[:], d["c_sgn"], w=["T_sgn"])
        cos = S.sbuf("T_cos", [64, BLK], F32)
        sin = S.sbuf("T_sin", [64, BLK], F32)
        tmpi = S.sbuf("T_tmpi", [64, BLK], I32)
        tmpf = [S.sbuf("T_tmpf%d" % i, [64, BLK], F32) for i in range(3)]
        t1 = S.sbuf("T_t1", [64, BLK], F32)
        t2 = S.sbuf("T_t2", [64, BLK], F32)
        sm = S.sbuf("T_sm", [128, 4], F32)
        osb = [S.sbuf("T_o%d" % i, [128, 128], F32) for i in range(2)]
        pf = [S.psum("T_pf%d" % i, [128, 512], F32) for i in range(5)]
        pb = [S.psum("T_pb%d" % i, [128, 1024], BF16) for i in range(2)]
        kb.memset("pool", v[:, :, 128:129], 1.0, w=["T_vone"])
        kb.ld(KR[:], d["krT"], w=["T_KR"])

        def loadw(dst, dkey, src, KCn, M):
            kb.ld(wst[:, :KCn, :M], src, w=["T_wst"])
            kb.copy(kb.rr_eng(), dst[:, :KCn, :M], wst[:, :KCn, :M], r=["T_wst"], w=[dkey])

        for h in range(HPC):
            loadw(wq, "T_wq", d["wq_n"][h], QKC, 128)
            loadw(wr1, "T_wr1", d["wq_r1"][h], QKC, 64)
            loadw(wr2, "T_wr2", d["wq_r2"][h], QKC, 64)
            loadw(wk, "T_wk", d["wk"][h], KVC, 128)
            loadw(wv, "T_wv", d["wv"][h], KVC, 128)
            for blk in range(SEQ // BLK if not (h > 0 and cfg.get("dbg_noproj")) else 0):
                t0 = blk * BLK
                kb.ld(cqb[:], d["cqnT"][blk], w=["T_cqb"])
                kb.ld(ckb[:], d["ckvnT"][blk], w=["T_ckb"])
                rope_tables(kb, cfg, d["pos"], t0, BLK, cos, sin, invf64, tmpi, tmpf, "T_rt", P=64)
                kb.ts("dve", cos[:, :], cos[:, :], sc, ALU.mult, r=["T_rt_c"], w=["T_rt_c"])
                kb.ts("dve", sin[:, :], sin[:, :], sgn[:, 0:1], ALU.mult, r=["T_rt_s", "T_sgn"], w=["T_rt_s"], s2=sc, op1=ALU.mult)
                kb.mm_group(pf[0][:, :BLK], [(wq[:, kc, :], cqb[:, kc, :]) for kc in range(QKC)], r=["T_wq", "T_cqb"], w=["T_pf0"])
                kb.act(qn[:, t0:t0 + BLK], pf[0][:, :BLK], AF.Copy, r=["T_pf0"], w=["T_qn"], scale=sc)
                kb.mm_group(pf[1][:64, :BLK], [(wr1[:, kc, :], cqb[:, kc, :]) for kc in range(QKC)], r=["T_wr1", "T_cqb"], w=["T_pf1"])
                kb.mm_group(pf[2][:64, :BLK], [(wr2[:, kc, :], cqb[:, kc, :]) for kc in range(QKC)], r=["T_wr2", "T_cqb"], w=["T_pf2"])
                kb.tt("dve", t1[:, :], pf[1][:64, :BLK], cos[:, :], ALU.mult, r=["T_pf1", "T_rt_c"], w=["T_t1"])
                kb.tt("dve", t2[:, :], pf[2][:64, :BLK], sin[:, :], ALU.mult, r=["T_pf2", "T_rt_s"], w=["T_t2"])
                kb.tt("pool", QR[:, t0:t0 + BLK], t1[:, :], t2[:, :], ALU.add, r=["T_t1", "T_t2"], w=["T_QR"])
                kb.mm_group(pf[3][:, :BLK], [(wk[:, kc, :], ckb[:, kc, :]) for kc in range(KVC)], r=["T_wk", "T_ckb"], w=["T_pf3"])
                kb.copy("act", kn[:, t0:t0 + BLK], pf[3][:, :BLK], r=["T_pf3"], w=["T_kn"])
                for j in range(BLK // 128):
                    kb.mm_group(pf[4][:, :128], [(ckb[:, kc, j * 128:(j + 1) * 128], wv[:, kc, :]) for kc in range(KVC)],
                                r=["T_wv", "T_ckb"], w=["T_pf4"])
                    kb.copy("dve", v[:, blk * (BLK // 128) + j, 0:128], pf[4][:, :128], r=["T_pf4"], w=["T_v"])
            for i in range(NT if not (h > 0 and cfg.get("dbg_noattn")) else 0):
                nk = (i + 1) * 128
                q0 = i * 128
                nblk = (nk + BLK - 1) // BLK
                for b_ in range(nblk):
                    k0 = b_ * BLK
                    n = min(BLK, nk - k0)
                    sp, spk = pf[b_ % 2], "T_pf%d" % (b_ % 2)
                    kb.mm_group(sp[:, :n], [(qn[:, q0:q0 + 128], kn[:, k0:k0 + n]), (QR[:, q0:q0 + 128], KR[:, k0:k0 + n])],
                                r=["T_qn", "T_kn", "T_QR", "T_KR"], w=[spk])
                    if b_ == nblk - 1:
                        if n > 128:
                            kb.copy(kb.rr_eng(), Ssb[:, k0:k0 + n - 128], sp[:, :n - 128], r=[spk], w=["T_S"])
                        kb.tt("dve", Ssb[:, nk - 128:nk], sp[:, n - 128:n], mask[:], ALU.add, r=[spk, "T_mask"], w=["T_S"])
                    else:
                        kb.copy(kb.rr_eng(), Ssb[:, k0:k0 + n], sp[:, :n], r=[spk], w=["T_S"])
                skeys = ["T_S"]
                kb.S.op("dve", lambda e, nk=nk: e.reduce_max(out=sm[:, 0:1], in_=Ssb[:, :nk], axis=AX.X), r=skeys, w=["T_mx"])
                kb.ts("dve", sm[:, 1:2], sm[:, 0:1], -1.0, ALU.mult, r=["T_mx"], w=["T_nmx"])
                kb.act(Psb[:, :nk], Ssb[:, :nk], AF.Exp, r=skeys + ["T_nmx"], w=["T_P"], bias=sm[:, 1:2])
                for g in range((i + 4) // 4):
                    nt_ = min(4, i + 1 - 4 * g)
                    pt, ptk = pb[g % 2], "T_pb%d" % (g % 2)
                    pe_seq(kb, [(lambda e, g=g, j=j, pt=pt: e.transpose(out=pt[:, j * 128:(j + 1) * 128],
                                                                      in_=Psb[:, (4 * g + j) * 128:(4 * g + j + 1) * 128], identity=ident[:]))
                                for j in range(nt_)], r=["T_P", "T_ident"], w=[ptk])
                    kb.copy(kb.rr_eng(("act", "dve", "pool")[:2]), PT[:, 4 * g * 128:(4 * g + nt_) * 128], pt[:, :nt_ * 128], r=[ptk], w=["T_PT%d" % g])
                ptkeys = ["T_PT%d" % g for g in range((i + 4) // 4)]
                kb.mm_group(pf[2][:, :129], [(PT[:, kt * 128:(kt + 1) * 128], v[:, kt, :]) for kt in range(i + 1)],
                            r=ptkeys + ["T_v", "T_vone"], w=["T_pf2"])
                kb.S.op("dve", lambda e: e.reciprocal(out=sm[:, 2:3], in_=pf[2][:, 128:129]), r=["T_pf2"], w=["T_rs"])
                o, ok = osb[i % 2], "T_o%d" % (i % 2)
                kb.ts("dve", o[:], pf[2][:, :128], sm[:, 2:3], ALU.mult, r=["T_pf2", "T_rs"], w=[ok])
                kb.st(d["oc"][h, q0:q0 + 128, :], o[:], r=[ok])


def build_B(cfg):
    nc = bass.Bass("TRN2", target_bir_lowering=False)
    kb = KB(nc)
    SEQ, NC = cfg["S"], cfg["NC"]
    CPC, LPC, HPC = cfg["DC"] // NC, cfg["DL"] // NC, cfg["H"] // NC
    QL, KVL = cfg["QL"], cfg["KVL"]
    QKC, KVC = QL // 128, KVL // 128
    nb = LPC // 128
    d = {
        "gTc": kb.dram_in("gTc", [CPC, SEQ]), "convw": kb.dram_in("convw", [128, CPC // 128, 31]), "convb": kb.dram_in("convb", [128, CPC // 128]),
        "yconvTc": kb.dram_out("yconvTc", [CPC, SEQ]),
        "uTc": kb.dram_in("uTc", [LPC, SEQ]), "lruw": kb.dram_in("lruw", [128, nb, 4]), "lrub": kb.dram_in("lrub", [128, nb]),
        "ba": kb.dram_in("ba", [128, nb]), "bx": kb.dram_in("bx", [128, nb]), "lam": kb.dram_in("lam", [128, nb]),
        "lwa": kb.dram_in("lwa", [nb, 128, 128]), "lwx": kb.dram_in("lwx", [nb, 128, 128]),
        "hlruTc": kb.dram_out("hlruTc", [LPC, SEQ]),
        "cqnT": kb.dram_in("cqnT", [SEQ // 512, 128, QKC, 512], BF16), "ckvnT": kb.dram_in("ckvnT", [SEQ // 512, 128, KVC, 512], BF16), "krT": kb.dram_in("krT", [64, SEQ], BF16),
        "pos": kb.dram_in("pos", [1, SEQ], I32),
        "wq_n": kb.dram_in("wq_n", [HPC, 128, QKC, 128]), "wq_r1": kb.dram_in("wq_r1", [HPC, 128, QKC, 64]),
        "wq_r2": kb.dram_in("wq_r2", [HPC, 128, QKC, 64]), "wk": kb.dram_in("wk", [HPC, 128, KVC, 128]), "wv": kb.dram_in("wv", [HPC, 128, KVC, 128]),
        "c_ident": kb.dram_in("c_ident", [128, 128], BF16), "c_mask": kb.dram_in("c_mask", [128, 128]),
        "c_invf64": kb.dram_in("c_invf64", [64, 1]), "c_sgn": kb.dram_in("c_sgn", [64, 1]),
        "oc": kb.dram_out("oc", [HPC, SEQ, 128]),
    }
    parts = cfg.get("bparts", "cla")
    if "c" in parts:
        phase_B_conv(kb, cfg, d)
    if "l" in parts:
        phase_B_lru(kb, cfg, d)
    if "a" in parts:
        phase_B_attn(kb, cfg, d)
    kb.S.emit()
    return nc


def blockify(aT, blk=512):
    F, S = aT.shape
    return np.ascontiguousarray(aT.reshape(F // 128, 128, S // blk, blk).transpose(2, 1, 0, 3))


def prep_B(cfg, p, c):
    NC = cfg["NC"]
    CPC, LPC, HPC = cfg["DC"] // NC, cfg["DL"] // NC, cfg["H"] // NC
    nb = LPC // 128
    cs_, ls_ = slice(c * CPC, (c + 1) * CPC), slice(c * LPC, (c + 1) * LPC)
    invf = consts()["c_invf"]
    d = {"convw": np.ascontiguousarray(p["conv_dw_w"][:, cs_].T.reshape(CPC // 128, 128, 31).transpose(1, 0, 2)),
         "convb": pvec(p["conv_dw_b"][cs_]),
         "lruw": np.ascontiguousarray(p["lru_conv_w"][:, ls_].T.reshape(nb, 128, 4).transpose(1, 0, 2)),
         "lrub": pvec(p["lru_conv_b"][ls_]), "ba": pvec(p["lru_b_a"][ls_]), "bx": pvec(p["lru_b_x"][ls_]), "lam": pvec(p["lru_lambda"][ls_]),
         "lwa": np.ascontiguousarray(p["lru_w_a"][c * nb:(c + 1) * nb]), "lwx": np.ascontiguousarray(p["lru_w_x"][c * nb:(c + 1) * nb]),
         "c_ident": np.eye(128, dtype=np.float32).astype(NPBF),
         "c_mask": np.where(np.arange(128)[None, :] <= np.arange(128)[:, None], 0.0, NEG).astype(np.float32),
         "c_invf64": np.concatenate([invf, invf], 0), "c_sgn": np.concatenate([-np.ones((32, 1)), np.ones((32, 1))], 0).astype(np.float32)}
    wqn, wr1, wr2, wk, wv = [], [], [], [], []
    for hh in range(c * HPC, (c + 1) * HPC):
        q0 = hh * 192
        wqn.append(chunkify(p["w_uq"][:, q0:q0 + 128], 128)[0])
        r = p["w_uq"][:, q0 + 128:q0 + 192]
        wr1.append(chunkify(r, 64)[0])
        wr2.append(chunkify(np.concatenate([r[:, 32:], r[:, :32]], 1), 64)[0])
        k0 = hh * 256
        wk.append(chunkify(p["w_ukv"][:, k0:k0 + 128], 128)[0])
        wv.append(chunkify(p["w_ukv"][:, k0 + 128:k0 + 256], 128)[0])
    d.update({"wq_n": np.stack(wqn), "wq_r1": np.stack(wr1), "wq_r2": np.stack(wr2), "wk": np.stack(wk), "wv": np.stack(wv)})
    return d


LAYER_KEYS = ("pre_norm_g", "w_in", "conv_dw_w", "conv_dw_b", "conv_ln_g", "conv_ln_b", "w_conv_proj", "q_norm_g", "w_uq",
              "kv_norm_g", "w_ukv", "w_mla_proj", "lru_conv_w", "lru_conv_b", "lru_w_a", "lru_b_a", "lru_w_x", "lru_b_x",
              "lru_lambda", "w_lru_proj", "w_out", "post_norm_g")

FULL_CFG = dict(NC=8, S=8192, T=1024, TP=512, TPC=512, D=4096, DC=2048, QL=1536, KVL=512, H=32, DL=2048)

_PROGS = {}


def _prog(name, cfg):
    key = (name, tuple(sorted(cfg.items())))
    if key not in _PROGS:
        _PROGS[key] = {"A": build_A, "B": build_B, "C": build_C}[name](cfg)
    return _PROGS[key]


def run_model(cfg, inp, depth, runner=None):
    runner = runner or (lambda nc, maps: run(nc, maps, cfg["NC"]))
    NC, S, T = cfg["NC"], cfg["S"], cfg["T"]
    CPC, LPC = cfg["DC"] // NC, cfg["DL"] // NC
    ca = np.ascontiguousarray
    x = np.asarray(inp["x"]).reshape(S, cfg["D"])
    pos = ca(np.asarray(inp["positions"]).reshape(1, S).astype(np.int32))
    xT = ca(x.T)
    for l in range(depth):
        p = {k: np.asarray(inp[k][l]) for k in LAYER_KEYS}
        xs = [ca(xT[:, c * T:(c + 1) * T]) for c in range(NC)]
        base = prep_A(cfg, p["w_in"], p["pre_norm_g"], p["q_norm_g"], p["kv_norm_g"])
        res = runner(_prog("A", cfg), [dict(base, xT=xs[c], pos=ca(pos[:, c * T:(c + 1) * T])) for c in range(NC)])
        del base
        cat1 = lambda n: ca(np.concatenate([np.asarray(r[n]) for r in res], 1))
        gT, uT, cqnT, ckvnT = cat1("gT"), cat1("uT"), cat1("cqnT"), cat1("ckvnT")
        krT = ca(np.concatenate([cat1("kr1T"), cat1("kr2T")], 0))
        cqnT, ckvnT = blockify(cqnT), blockify(ckvnT)
        maps = [dict(prep_B(cfg, p, c), gTc=ca(gT[c * CPC:(c + 1) * CPC]), uTc=ca(uT[c * LPC:(c + 1) * LPC]),
                     cqnT=cqnT, ckvnT=ckvnT, krT=krT, pos=pos) for c in range(NC)]
        res = runner(_prog("B", cfg), maps)
        del maps, gT, uT
        yconvT = ca(np.concatenate([np.asarray(r["yconvTc"]) for r in res], 0))
        hlruT = ca(np.concatenate([np.asarray(r["hlruTc"]) for r in res], 0))
        oT = ca(np.concatenate([np.asarray(r["oc"]) for r in res], 0).transpose(0, 2, 1).reshape(-1, S))
        base = prep_C(cfg, p)
        sl = lambda a, c: ca(a[:, c * T:(c + 1) * T])
        res = runner(_prog("C", cfg), [dict(base, xT=xs[c], yconvT=sl(yconvT, c), oT=sl(oT, c), hlruT=sl(hlruT, c)) for c in range(NC)])
        del base
        xT = ca(np.concatenate([np.asarray(r["xTn"]) for r in res], 1))
    return ca(xT.T).astype(np.float32)


def kernel(**inputs):
    out = run_model(FULL_CFG, inputs, 4)
    return out.reshape(1, FULL_CFG["S"], FULL_CFG["D"])
```
